# Optimizing a Trainium2 kernel written in Bass

```python
import jax, jax.numpy as jnp
from jax import lax
import numpy as np

D_MODEL = 1024
BATCH = 16
SEQ = 2048
DEPTH = 1

CHUNK = 64
CONV_CH = D_MODEL
CONV_WIDTH = 31
RET_HEADS = 4
RET_DK = D_MODEL // RET_HEADS
RET_DV = 2 * RET_DK
RET_QK = RET_HEADS * RET_DK
RET_V = RET_HEADS * RET_DV
FFN_HIDDEN = 2816
FFN_CONV_WIDTH = 3
PLE_DIM = 256
ROPE_BASE = 10000.0
LN_EPS = 1e-5
DEEPNORM_ALPHA = (2.0 * DEPTH) ** 0.25
DEEPNORM_BETA = (8.0 * DEPTH) ** -0.25

SPLIT_IDX = (
    CONV_CH,
    2 * CONV_CH,
    2 * CONV_CH + RET_QK,
    2 * CONV_CH + 2 * RET_QK,
    2 * CONV_CH + 2 * RET_QK + RET_V,
    2 * CONV_CH + 2 * RET_QK + 2 * RET_V,
    2 * CONV_CH + 2 * RET_QK + 2 * RET_V + D_MODEL,
)
IN_COLS = 2 * CONV_CH + 2 * RET_QK + 2 * RET_V + 2 * D_MODEL

kernel_name = "hybrid_conformer_conv_retention_deepnorm_block"


def layer_norm(x, g, b):
    xf = x.astype(jnp.float32)
    mu = jnp.mean(xf, axis=-1, keepdims=True)
    var = jnp.mean(jnp.square(xf - mu), axis=-1, keepdims=True)
    return ((xf - mu) * lax.rsqrt(var + LN_EPS) * g + b).astype(x.dtype)


def head_group_norm(y, g):
    yf = y.astype(jnp.float32)
    mu = jnp.mean(yf, axis=-1, keepdims=True)
    var = jnp.mean(jnp.square(yf - mu), axis=-1, keepdims=True)
    return ((yf - mu) * lax.rsqrt(var + LN_EPS) * g).astype(y.dtype)


def causal_depthwise_conv(x, w, b):
    k_width = w.shape[0]
    y = lax.conv_general_dilated(
        x, w[:, None, :].astype(x.dtype), window_strides=(1,),
        padding=[(k_width - 1, 0)], dimension_numbers=('NWC', 'WIO', 'NWC'),
        feature_group_count=x.shape[-1])
    return y + b.astype(x.dtype)


def rotary(x, pos):
    half = x.shape[-1] // 2
    inv = ROPE_BASE ** (-jnp.arange(half, dtype=jnp.float32) / half)
    ang = pos.astype(jnp.float32)[..., None] * inv
    cos = jnp.cos(ang)[:, :, None, :]
    sin = jnp.sin(ang)[:, :, None, :]
    x1 = x[..., :half].astype(jnp.float32)
    x2 = x[..., half:].astype(jnp.float32)
    out = jnp.concatenate([x1 * cos - x2 * sin, x2 * cos + x1 * sin], axis=-1)
    return out.astype(x.dtype)


def chunk_retention(q, k, v, log_gamma):
    bsz, seq, nh, dk = q.shape
    dv = v.shape[-1]
    n_chunks = seq // CHUNK
    qc = q.reshape(bsz, n_chunks, CHUNK, nh, dk)
    kc = k.reshape(bsz, n_chunks, CHUNK, nh, dk)
    vc = v.reshape(bsz, n_chunks, CHUNK, nh, dv)
    idx = jnp.arange(CHUNK, dtype=jnp.float32)
    d_intra = jnp.exp(log_gamma[:, None, None] * jnp.abs(idx[:, None] - idx[None, :]))
    scores = jnp.einsum('bnihd,bnjhd->bnhij', qc, kc) * d_intra.astype(q.dtype)
    y_intra = jnp.einsum('bnhij,bnjhe->bnihe', scores, vc)
    xi = jnp.exp(log_gamma[None, :] * (idx[:, None] + 1.0))
    zeta = jnp.exp(log_gamma[None, :] * (CHUNK - 1.0 - idx[:, None]))
    gamma_chunk = jnp.exp(log_gamma * CHUNK)

    def step(state, inp):
        qi, ki, vi = inp
        y = jnp.einsum('bihd,bhde->bihe', qi * xi[None, :, :, None], state)
        state = state * gamma_chunk[None, :, None, None] + jnp.einsum(
            'bjhd,bjhe->bhde', ki * zeta[None, :, :, None], vi)
        return state, y

    state0 = jnp.zeros((bsz, nh, dk, dv), jnp.float32)
    xs = (jnp.moveaxis(qc, 1, 0), jnp.moveaxis(kc, 1, 0), jnp.moveaxis(vc, 1, 0))
    _, y_inter = lax.scan(step, state0, xs)
    y = y_intra + jnp.moveaxis(y_inter, 0, 1).astype(y_intra.dtype)
    return y.reshape(bsz, seq, nh, dv)


def _xavier(key, shape, gain):
    fan_in, fan_out = shape[-2], shape[-1]
    return jax.random.normal(key, shape, jnp.float32) * (gain * (2.0 / (fan_in + fan_out)) ** 0.5)


def setup_inputs(seed: int = 0) -> dict:
    key = jax.random.key(seed)
    ks = jax.random.split(key, 32)
    f32 = jnp.float32
    L = DEPTH
    x = jax.random.normal(ks[0], (BATCH, SEQ, D_MODEL), f32)
    offsets = jax.random.randint(ks[1], (BATCH, 1), 0, 64) * CHUNK
    positions = (offsets + jnp.arange(SEQ, dtype=jnp.int32)[None, :]).astype(jnp.int32)
    p = jax.random.normal(ks[2], (DEPTH, BATCH, SEQ, PLE_DIM), f32)
    beta = DEEPNORM_BETA

    def gain(k, shape):
        return 1.0 + 0.05 * jax.random.normal(k, shape, f32)

    def small(k, shape):
        return 0.02 * jax.random.normal(k, shape, f32)

    return {
        'x': x,
        'positions': positions,
        'p': p,
        'ln0_g': gain(ks[3], (D_MODEL,)),
        'ln0_b': small(ks[4], (D_MODEL,)),
        'w_in': jax.random.normal(ks[5], (L, D_MODEL, IN_COLS), f32) * D_MODEL ** -0.5,
        'b_gate': small(ks[6], (L, 2, D_MODEL)),
        'conv_dw_w': jax.random.normal(ks[7], (L, CONV_WIDTH, CONV_CH), f32) * CONV_WIDTH ** -0.5,
        'conv_dw_b': small(ks[8], (L, CONV_CH)),
        'conv_ln_g': gain(ks[9], (L, CONV_CH)),
        'conv_ln_b': small(ks[10], (L, CONV_CH)),
        'w_conv_o': _xavier(ks[11], (L, CONV_CH, D_MODEL), beta),
        'b_conv_o': small(ks[12], (L, D_MODEL)),
        'ret_gn_g': gain(ks[13], (L, RET_HEADS, RET_DV)),
        'w_ret_o': _xavier(ks[14], (L, RET_V, D_MODEL), beta),
        'w_out': _xavier(ks[15], (L, D_MODEL, D_MODEL), beta),
        'ln1_g': gain(ks[16], (L, D_MODEL)),
        'ln1_b': small(ks[17], (L, D_MODEL)),
        'w_ffn_up': _xavier(ks[18], (L, D_MODEL, 2 * FFN_HIDDEN), beta),
        'ffn_dw_w': jax.random.normal(ks[19], (L, FFN_CONV_WIDTH, 2 * FFN_HIDDEN), f32) * FFN_CONV_WIDTH ** -0.5,
        'ffn_dw_b': small(ks[20], (L, 2 * FFN_HIDDEN)),
        'w_ffn_down': _xavier(ks[21], (L, FFN_HIDDEN, D_MODEL), beta),
        'w_ple': _xavier(ks[22], (L, PLE_DIM, D_MODEL), beta),
        'w_ple_gate': jax.random.normal(ks[23], (L, D_MODEL, D_MODEL), f32) * D_MODEL ** -0.5,
        'b_ple_gate': small(ks[24], (L, D_MODEL)),
        'ln2_g': gain(ks[25], (L, D_MODEL)),
        'ln2_b': small(ks[26], (L, D_MODEL)),
    }


def reference(x, positions, p, ln0_g, ln0_b, w_in, b_gate, conv_dw_w, conv_dw_b,
              conv_ln_g, conv_ln_b, w_conv_o, b_conv_o, ret_gn_g, w_ret_o, w_out,
              ln1_g, ln1_b, w_ffn_up, ffn_dw_w, ffn_dw_b, w_ffn_down, w_ple,
              w_ple_gate, b_ple_gate, ln2_g, ln2_b):
    bsz, seq, _ = x.shape
    log_gamma = jnp.log1p(-jnp.exp2(-5.0 - jnp.arange(RET_HEADS, dtype=jnp.float32)))
    h = layer_norm(x, ln0_g, ln0_b)
    for i in range(DEPTH):
        z = h @ w_in[i]
        cv, cg, q, k, v, g, ga, gb = jnp.split(z, SPLIT_IDX, axis=-1)
        u = cv * jax.nn.sigmoid(cg)
        u = causal_depthwise_conv(u, conv_dw_w[i], conv_dw_b[i])
        u = jax.nn.silu(layer_norm(u, conv_ln_g[i], conv_ln_b[i]))
        y_conv = u @ w_conv_o[i] + b_conv_o[i]
        q = rotary(q.reshape(bsz, seq, RET_HEADS, RET_DK), positions)
        k = rotary(k.reshape(bsz, seq, RET_HEADS, RET_DK), positions) * (RET_DK ** -0.5)
        v = v.reshape(bsz, seq, RET_HEADS, RET_DV)
        r = chunk_retention(q, k, v, log_gamma)
        r = head_group_norm(r, ret_gn_g[i]).reshape(bsz, seq, RET_V)
        y_ret = (jax.nn.silu(g) * r) @ w_ret_o[i]
        mix = jax.nn.sigmoid(ga + b_gate[i, 0]) * y_conv + jax.nn.sigmoid(gb + b_gate[i, 1]) * y_ret
        h = layer_norm(DEEPNORM_ALPHA * h + mix @ w_out[i], ln1_g[i], ln1_b[i])
        a = causal_depthwise_conv(h @ w_ffn_up[i], ffn_dw_w[i], ffn_dw_b[i])
        a_gate, a_val = jnp.split(a, 2, axis=-1)
        f = (jax.nn.silu(a_gate) * a_val) @ w_ffn_down[i]
        e = (p[i] @ w_ple[i]) * jax.nn.sigmoid(h @ w_ple_gate[i] + b_ple_gate[i])
        h = layer_norm(DEEPNORM_ALPHA * h + f + e, ln2_g[i], ln2_b[i])
    return h
```

```python
import contextlib
import numpy as np
import concourse.bass as bass
import concourse.mybir as mybir
from concourse.bass_utils import run_bass_kernel_spmd

F32 = mybir.dt.float32
BF16 = mybir.dt.bfloat16
I32 = mybir.dt.int32
AF = mybir.ActivationFunctionType
ALU = mybir.AluOpType

NCORES = 8
D = 1024
SEQ = 2048
TB = 512
NBLK = 8
TOK = 4096
H = 4
FFN = 2816
NJ = 22
EPS = 1e-5
ALPHA = 2.0 ** 0.25
COMPUTE = ("pe", "act", "dve", "pool")


class Buf:
    __slots__ = ("name", "last_w", "readers", "excl")

    def __init__(self, name, excl=False):
        self.name = name
        self.last_w = None
        self.readers = []
        self.excl = excl


def alias_barrier(old_bufs, new_bufs):
    toks = []
    for b in old_bufs:
        if b.last_w is not None:
            toks.append(b.last_w)
        toks.extend(b.readers)
    for nb in new_bufs:
        nb.readers.extend(toks)


class Sched:
    def __init__(self, nc):
        self.nc = nc
        self.ops = {e: [] for e in ("pe", "act", "dve", "pool", "sp")}
        self.dma_count = {}
        self.known = {e: {} for e in self.ops}

    def _waits(self, eng, reads, writes, is_dma):
        deps = []
        for b in reads:
            if b.last_w is not None:
                deps.append(b.last_w)
        for b in writes:
            if b.last_w is not None:
                deps.append(b.last_w)
            for r in b.readers:
                deps.append(r)
        out = {}
        for (key, val, kind) in deps:
            if (not is_dma) and kind == "c" and key == eng and eng == "pe":
                continue
            k = (key, kind)
            if k not in out or out[k] < val:
                out[k] = val
        waits = []
        kn = self.known[eng]
        for (key, kind), val in out.items():
            if kn.get((key, kind), -1) >= val:
                continue
            kn[(key, kind)] = val
            waits.append((key, val, kind))
        return waits

    @staticmethod
    def _finish(tok, reads, writes):
        for b in reads:
            b.readers.append(tok)
        for b in writes:
            b.last_w = tok
            b.readers = []

    def op(self, eng, fn, reads=(), writes=()):
        reads = [b for b in reads if b is not None]
        writes = [b for b in writes if b is not None]
        writes = writes + [b for b in reads if b.excl and b not in writes]
        waits = self._waits(eng, reads, writes, False)
        tok = (eng, len(self.ops[eng]), "c")
        self.ops[eng].append({"fn": fn, "waits": waits, "inc": None, "need": False})
        self._finish(tok, reads, writes)
        return tok

    def dma(self, q, sem, fn, reads=(), writes=()):
        reads = [b for b in reads if b is not None]
        writes = [b for b in writes if b is not None]
        waits = self._waits(q, reads, writes, True)
        n = self.dma_count.get(sem, 0) + 1
        self.dma_count[sem] = n
        tok = (sem, 16 * n, "d")
        self.ops[q].append({"fn": fn, "waits": waits, "inc": (sem, 16), "need": True})
        self._finish(tok, reads, writes)
        return tok

    def wait_all(self, eng, toks):
        self.ops[eng].append({"fn": None, "waits": list(toks), "inc": None, "need": False})

    def emit(self):
        nc = self.nc
        for e, lst in self.ops.items():
            for o in lst:
                for (key, val, kind) in o["waits"]:
                    if kind == "c":
                        self.ops[key][val]["need"] = True
        NS = 8
        cum = {}
        for e in COMPUTE:
            c = [0] * NS
            arr = []
            for i, o in enumerate(self.ops[e]):
                if o["need"] and o["inc"] is None:
                    c[i % NS] += 1
                arr.append(c[i % NS])
            cum[e] = arr
        with contextlib.ExitStack() as st:
            sems = {}
            for e in COMPUTE:
                if any(o["need"] and o["inc"] is None for o in self.ops[e]):
                    sems[e] = [st.enter_context(nc.semaphore("s_%s%d" % (e, i))) for i in range(NS)]
            for s in self.dma_count:
                sems[s] = st.enter_context(nc.semaphore("d_" + s))
            block = st.enter_context(nc.Block())

            def run(engname):
                def body(eng):
                    for i, o in enumerate(self.ops[engname]):
                        for (key, val, kind) in o["waits"]:
                            if kind == "c":
                                eng.wait_ge(sems[key][val % NS], cum[key][val])
                            else:
                                eng.wait_ge(sems[key], val)
                        if o["fn"] is None:
                            continue
                        ins = o["fn"](eng)
                        if o["inc"] is not None:
                            ins.then_inc(sems[o["inc"][0]], o["inc"][1])
                        elif o["need"]:
                            ins.then_inc(sems[engname][i % NS], 1)
                return body

            block.tensor(run("pe"))
            block.scalar(run("act"))
            block.vector(run("dve"))
            block.gpsimd(run("pool"))
            block.sync(run("sp"))


WSPEC = {
    "w_in": (8, 256, 40),
    "w_conv_o": (8, 256, 4),
    "w_out": (8, 256, 4),
    "w_ple_gate": (8, 256, 4),
    "w_ffn_up": (8, 256, 22),
    "w_ret_o": (16, 128, 8),
    "w_ffn_down": (11, 128, 16),
    "w_ple": (2, 256, 4),
}


def _pack_w(w, kt, c):
    K, N = w.shape
    nch = N // c
    a = w.reshape(kt, 128, nch, c)
    return np.ascontiguousarray(a.transpose(2, 1, 0, 3)).reshape(nch, 128, kt * c)


def _pack_wdown(w):
    a = w.reshape(2, 11, 128, 8, 128)
    return np.ascontiguousarray(a.transpose(3, 0, 2, 1, 4)).reshape(16, 128, 11 * 128)


PCOLS = {}


def _pcol(name, n, _state=[0]):
    PCOLS[name] = (_state[0], n)
    _state[0] += n


for _n, _w in [("ln0_g", 8), ("ln0_b", 8), ("bg0", 8), ("bg1", 8), ("cdw", 8 * 31), ("cdb", 8), ("clg", 8), ("clb", 8),
               ("bco", 8), ("gng", 16), ("ln1_g", 8), ("ln1_b", 8), ("fdw", 44 * 3), ("fdb", 44), ("bpg", 8),
               ("ln2_g", 8), ("ln2_b", 8), ("zs", 4), ("epsx", 4), ("invf", 1), ("eps", 1), ("halfpi", 1)]:
    _pcol(_n, _w)
NP = sum(v[1] for v in PCOLS.values())


def _vec(v):
    return np.ascontiguousarray(np.asarray(v, np.float32).reshape(-1, 128).T)


def _consts():
    gam = 1.0 - 2.0 ** (-5.0 - np.arange(H, dtype=np.float64))
    lg = np.log(gam)
    idx = np.arange(128, dtype=np.float64)
    i = idx[None, :]
    j = idx[:, None]
    ci = (i // 64)
    cj = (j // 64)
    mk = np.zeros((128, H, 128), np.float64)
    for h in range(H):
        w = np.where(ci == cj, np.exp(lg[h] * np.abs(i - j)), np.where(ci > cj, np.exp(lg[h] * (i - j)), 0.0))
        w = w * np.exp(-lg[h] * (i + 1.0)) / 16.0
        mk[:, h, :] = w
    zs = np.exp(lg[None, :] * (127.0 - idx[:, None])) / 16.0
    epsx = EPS * np.exp(-2.0 * lg[None, :] * (idx[:, None] + 1.0))
    gc = np.exp(lg * 128.0)
    invf = 10000.0 ** (-np.arange(128, dtype=np.float32) / np.float32(128))
    return mk.astype(np.float32), zs.astype(np.float32), epsx.astype(np.float32), [float(g) for g in gc], invf.astype(np.float32)


def _pack_params(inp):
    P = np.zeros((128, NP), np.float32)

    def put(name, arr):
        o, n = PCOLS[name]
        assert arr.shape == (128, n), (name, arr.shape, n)
        P[:, o:o + n] = arr

    put("ln0_g", _vec(inp["ln0_g"]))
    put("ln0_b", _vec(inp["ln0_b"]))
    put("bg0", _vec(inp["b_gate"][0, 0]))
    put("bg1", _vec(inp["b_gate"][0, 1]))
    cw = np.asarray(inp["conv_dw_w"][0], np.float32)
    put("cdw", np.ascontiguousarray(cw.reshape(31, 8, 128).transpose(2, 1, 0)).reshape(128, 8 * 31))
    put("cdb", _vec(inp["conv_dw_b"][0]))
    put("clg", _vec(inp["conv_ln_g"][0]))
    put("clb", _vec(inp["conv_ln_b"][0]))
    put("bco", _vec(inp["b_conv_o"][0]))
    put("gng", _vec(np.asarray(inp["ret_gn_g"][0]).reshape(-1)))
    put("ln1_g", _vec(inp["ln1_g"][0]))
    put("ln1_b", _vec(inp["ln1_b"][0]))
    fw = np.asarray(inp["ffn_dw_w"][0], np.float32)
    put("fdw", np.ascontiguousarray(fw.reshape(3, 44, 128).transpose(2, 1, 0)).reshape(128, 44 * 3))
    put("fdb", _vec(inp["ffn_dw_b"][0]))
    put("bpg", _vec(inp["b_ple_gate"][0]))
    put("ln2_g", _vec(inp["ln2_g"][0]))
    put("ln2_b", _vec(inp["ln2_b"][0]))
    mk, zs, epsx, gc, invf = _consts()
    put("zs", zs)
    put("epsx", epsx)
    put("invf", invf.reshape(128, 1))
    put("eps", np.full((128, 1), EPS, np.float32))
    put("halfpi", np.full((128, 1), np.pi / 2, np.float32))
    return P, mk


def build_nc(nblk=NBLK, stop=None):
    nc = bass.Bass("TRN2", target_bir_lowering=False)
    _, _, _, GC, _ = _consts()

    x_d = nc.dram_tensor("x", [TOK, D], F32, kind="ExternalInput").ap()
    pos_d = nc.dram_tensor("pos", [1, TOK], I32, kind="ExternalInput").ap()
    p_d = nc.dram_tensor("p", [TOK, 256], F32, kind="ExternalInput").ap()
    par_d = nc.dram_tensor("params", [128, NP], F32, kind="ExternalInput").ap()
    mk_d = nc.dram_tensor("mask", [128, H * 128], F32, kind="ExternalInput").ap()
    id_d = nc.dram_tensor("ident", [128, 128], F32, kind="ExternalInput").ap()
    wd = {}
    ws = {}
    for name, (kt, c, nch) in WSPEC.items():
        wd[name] = nc.dram_tensor(name, [nch, 128, kt * c], F32, kind="ExternalInput").ap()
        ws[name] = nc.dram_tensor(name + "_bf", [nch, 128, kt * c], BF16).ap()
    out_d = nc.dram_tensor("out", [TOK, D], F32, kind="ExternalOutput").ap()

    S = Sched(nc)
    with contextlib.ExitStack() as st:
        def sb(name, shape, dt):
            return st.enter_context(nc.sbuf_tensor(name, shape, dt))

        def ps(name, shape, dt):
            return st.enter_context(nc.psum_tensor(name, shape, dt))

        PAR = sb("PAR", [128, NP], F32)
        MK = sb("MK", [128, H, 128], F32)
        IDF = sb("IDF", [128, 128], F32)
        IDB = sb("IDB", [128, 128], BF16)
        ONF = sb("ONF", [128, 128], F32)
        ONB = sb("ONB", [128, 128], BF16)
        HT32 = sb("HT32", [128, 8, TB], F32)
        FA = sb("FA", [128, 8, TB], F32)
        FC = sb("FC", [128, 4, TB], F32)
        XS = sb("XS", [128, 2, D], F32)
        ST = sb("ST", [128, 3, TB], F32)
        CS = sb("CS", [128, 2, TB], F32)
        RT = sb("RT", [128, 2, TB], F32)
        RN = sb("RN", [128, 2, TB], F32)
        SF = sb("SF", [128, H, 2, 512], F32)
        PL = sb("PL", [128, 4, 256], F32)
        OS = sb("OS", [128, 2, D], F32)
        SMALL = sb("SMALL", [128, 128], F32)
        POSI = sb("POSI", [128, TB], I32)
        BA = sb("BA", [128, 8, TB], BF16)
        BB = sb("BB", [128, 8, TB], BF16)
        HID = sb("HID", [128, NJ * TB], BF16)
        UH = sb("UH", [128, 8, 30], BF16)
        DG = sb("DG", [128, 2, 31, 128], BF16)
        RTT = sb("RTT", [128, 16, TB], BF16)
        SBF = sb("SBF", [128, H, 2, 512], BF16)
        WR = sb("WR", [128, 4, 2048], BF16)
        PT = sb("PT", [128, 2, TB], BF16)
        SDT = sb("SDT", [128, 4, 128], BF16)
        ZHT = sb("ZHT", [128, 176], F32)
        PSF = [ps("PS%d" % i, [128, 512], F32) for i in range(7)]
        PSB = ps("PSB", [128, 1024], BF16)
        bPS = [Buf("ps%d" % i, True) for i in range(7)]
        bPSB = Buf("psb", True)
        ps_rr = [0]

        ps_n = [7]

        def next_ps():
            i = ps_rr[0] % ps_n[0]
            ps_rr[0] += 1
            return PSF[i], bPS[i]

        def py_ps(t):
            return PSF[5 + t % 2], bPS[5 + t % 2]

        def pc(name, f=None, n=1):
            o, w = PCOLS[name]
            if f is None:
                return PAR[:, o:o + w]
            return PAR[:, o + f:o + f + n]

        def mm(out, lhsT, rhs, start, stop, r, w):
            S.op("pe", lambda e: e.matmul(out, lhsT=lhsT, rhs=rhs, start=start, stop=stop), r, w)

        def tr(out, in_, ident, r, w):
            S.op("pe", lambda e: e.transpose(out, in_, ident), r, w)

        def act(out, in_, func, r, w, bias=None, scale=None):
            kw = {}
            if bias is not None:
                kw["bias"] = bias
            if scale is not None:
                kw["scale"] = scale
            S.op("act", lambda e: e.activation(out=out, in_=in_, func=func, **kw), r, w)

        def tt(out, in0, in1, op, r, w):
            S.op("dve", lambda e: e.tensor_tensor(out=out, in0=in0, in1=in1, op=op), r, w)

        def ts(out, in0, s1, s2, op0, op1, r, w):
            if s2 is None:
                S.op("dve", lambda e: e.tensor_scalar(out=out, in0=in0, scalar1=s1, scalar2=None, op0=op0), r, w)
            else:
                S.op("dve", lambda e: e.tensor_scalar(out=out, in0=in0, scalar1=s1, scalar2=s2, op0=op0, op1=op1), r, w)

        def stt(out, in0, scalar, in1, op0, op1, r, w):
            S.op("dve", lambda e: e.scalar_tensor_tensor(out=out, in0=in0, scalar=scalar, in1=in1, op0=op0, op1=op1), r, w)

        def cp(out, in_, r, w):
            S.op("dve", lambda e: e.tensor_copy(out=out, in_=in_), r, w)

        def dma(sem, out, in_, r, w):
            return S.dma("sp", sem, lambda e: e.dma_start(out=out, in_=in_), r, w)

        bPAR, bMK, bID, bON = Buf("par"), Buf("mk"), Buf("id"), Buf("on")
        bHT = [Buf("ht%d" % f) for f in range(8)]
        bFA = [Buf("fa%d" % f) for f in range(8)]
        bFC = [Buf("fc%d" % f) for f in range(4)]
        bXS = [Buf("xs%d" % i) for i in range(2)]
        bST = [Buf("st%d" % i) for i in range(3)]
        bCS, bRT, bRN = Buf("cs"), [Buf("rt0"), Buf("rt1")], [Buf("rn0"), Buf("rn1")]
        bSMA = [Buf("sma%d" % i) for i in range(4)]
        bSMR = [Buf("smr%d" % i) for i in range(4)]
        bSF = [[Buf("sf%d%d" % (h, d)) for d in range(2)] for h in range(H)]
        bSB = [[Buf("sb%d%d" % (h, d)) for d in range(2)] for h in range(H)]
        bPL, bOS, bSM, bPOSI = Buf("pl"), [Buf("os0"), Buf("os1")], Buf("small"), Buf("posi")
        bBA = [Buf("ba%d" % f) for f in range(8)]
        bBB = [Buf("bb%d" % f) for f in range(8)]
        bUH, bDG = Buf("uh"), [Buf("dg0"), Buf("dg1")]
        bRTT = [Buf("rtt%d" % i) for i in range(16)]
        bWR = [Buf("wr%d" % i) for i in range(4)]
        bPT = Buf("pt")
        bWS = {name: [Buf("ws_%s%d" % (name, i)) for i in range(WSPEC[name][2])] for name in WSPEC}
        bZH = [Buf("zh%d" % i) for i in range(44)]
        out_toks = {}
        bOUT = Buf("out")
        U = HID[:, 0:8 * 542].rearrange("p (f t) -> p f t", f=8)
        bU = [Buf("u%d" % f) for f in range(8)]
        QTs = [HID[:, 1024 * i:1024 * (i + 1)].rearrange("p (d t) -> p d t", d=2) for i in range(2)]
        KTs = [HID[:, 2048 + 1024 * i:2048 + 1024 * (i + 1)].rearrange("p (d t) -> p d t", d=2) for i in range(2)]
        KZ = HID[:, 4096:5120].rearrange("p (t d) -> p t d", t=4)
        VVs = [HID[:, 5120 + 2048 * i:5120 + 2048 * (i + 1)].rearrange("p (t e) -> p t e", t=4) for i in range(2)]
        RB_ = HID[:, 9216:11264].rearrange("p (t e) -> p t e", t=4)
        SD = SDT
        bQTs, bKTs, bVVs = [Buf("qt0"), Buf("qt1")], [Buf("kt0"), Buf("kt1")], [Buf("vv0"), Buf("vv1")]
        bKZ, bSD, bRB = Buf("kz"), Buf("sd"), Buf("rb")
        HIDV = HID[:, :].rearrange("p (j t) -> p j t", j=NJ)
        bHID = [Buf("hid%d" % j) for j in range(NJ)]
        hid_groups = {"u": bU, "head": bQTs + bKTs + bVVs + [bKZ, bRB], "hid": bHID}

        dma("c0", PAR[:], par_d, [], [bPAR])
        dma("c1", MK[:].rearrange("p h i -> p (h i)"), mk_d, [], [bMK])
        dma("c2", IDF[:], id_d, [], [bID])
        cp(IDB[:], IDF[:], [bID], [bID])
        S.op("dve", lambda e: e.memset(ONF[:], 1.0 / 1024), [], [bON])
        S.op("dve", lambda e: e.memset(ONB[:], 1.0 / 1024), [], [bON])

        stg_f = [FA[:, 0:4, :].rearrange("p a t -> p (a t)"), FA[:, 4:8, :].rearrange("p a t -> p (a t)"),
                 HT32[:, 0:4, :].rearrange("p a t -> p (a t)"), HT32[:, 4:8, :].rearrange("p a t -> p (a t)")]
        bstg_f = [Buf("stgf%d" % i) for i in range(4)]
        stg_b = [HID[:, 2048 * i:2048 * (i + 1)] for i in range(4)]
        bstg_b = [Buf("stgb%d" % i) for i in range(4)]
        order = ["w_in", "w_conv_o", "w_ret_o", "w_out", "w_ffn_up", "w_ple_gate", "w_ple", "w_ffn_down"]
        chunks = [(name, ci) for name in order for ci in range(WSPEC[name][2])]

        def pro_load(k):
            name, ci = chunks[k]
            kt, c, nch = WSPEC[name]
            s = k % 4
            dma("pl%d" % s, stg_f[s][:, 0:kt * c], wd[name][ci], [], [bstg_f[s]])

        LOOK = 3
        for k in range(min(LOOK, len(chunks))):
            pro_load(k)
        for k, (name, ci) in enumerate(chunks):
            kt, c, nch = WSPEC[name]
            n = kt * c
            s = k % 4
            if name == "w_ret_o":
                o_, _ = PCOLS["gng"]
                S.op("dve", (lambda sf, sbf: lambda e: e.tensor_tensor(
                    out=sbf.rearrange("p (k c) -> p k c", k=16), in0=sf.rearrange("p (k c) -> p k c", k=16),
                    in1=PAR[:, o_:o_ + 16].unsqueeze(2).broadcast_to([128, 16, 128]), op=ALU.mult))(stg_f[s][:, 0:n], stg_b[s][:, 0:n]),
                    [bstg_f[s], bPAR], [bstg_b[s]])
            elif k % 2 == 0:
                act(stg_b[s][:, 0:n], stg_f[s][:, 0:n], AF.Copy, [bstg_f[s]], [bstg_b[s]])
            else:
                cp(stg_b[s][:, 0:n], stg_f[s][:, 0:n], [bstg_f[s]], [bstg_b[s]])
            dma("ps%d" % s, ws[name][ci], stg_b[s][:, 0:n], [bstg_b[s]], [bWS[name][ci]])
            if k + LOOK < len(chunks):
                pro_load(k + LOOK)
        alias_barrier(bstg_f, bFA)
        alias_barrier(bstg_f, bHT)
        alias_barrier(bstg_b, bU + hid_groups["head"] + bHID)

        wr_rr = [0]

        def wload(name, ci):
            kt, c, nch = WSPEC[name]
            n = kt * c
            s = wr_rr[0] % 4
            wr_rr[0] += 1
            dma("w%d" % s, WR[:, s, 0:n], ws[name][ci], [bWS[name][ci]], [bWR[s]])
            return WR[:, s, 0:n].rearrange("p (k c) -> p k c", k=kt), bWR[s]

        def ln_stats(SRCB, bSRCB, SQ, bSQ):
            pm, bpm = next_ps()
            for f in range(8):
                mm(pm[:], ONB[:], SRCB[:, f, :], f == 0, f == 7, [bON, bSRCB[f]], [bpm])
            pe_, bpe = next_ps()
            for f in range(8):
                mm(pe_[:], ONB[:], SQ[:, f, :], f == 0, f == 7, [bON, bSQ[f]], [bpe])
            act(ST[:, 0, :], pm[:], AF.Copy, [bpm], [bST[0]])
            act(ST[:, 1, :], pm[:], AF.Square, [bpm], [bST[1]])
            tt(ST[:, 1, :], pe_[:], ST[:, 1, :], ALU.subtract, [bpe, bST[1]], [bST[1]])
            act(ST[:, 1, :], ST[:, 1, :], AF.Sqrt, [bST[1], bPAR], [bST[1]], bias=pc("eps", 0))
            S.op("dve", lambda e: e.reciprocal(out=ST[:, 1, :], in_=ST[:, 1, :]), [bST[1]], [bST[1]])

        def ln_norm(SRC, bSRC, f):
            tt(SRC[:, f, :], SRC[:, f, :], ST[:, 0, :], ALU.subtract, [bSRC[f], bST[0]], [bSRC[f]])
            tt(SRC[:, f, :], SRC[:, f, :], ST[:, 1, :], ALU.mult, [bSRC[f], bST[1]], [bSRC[f]])

        C1 = 6.28125
        C2 = 2 * np.pi - 6.28125

        for s_ in range(2):
            dma("x%d" % s_, XS[:, s_, :], x_d[s_ * 128:(s_ + 1) * 128, :], [], [bXS[s_]])
        for blk in range(nblk):
            t0 = blk * TB
            first = (blk % 4 == 0)
            if stop == 'P':
                break
            if stop == 'pos':
                break
            deferred_cp = []
            for tti in range(4):
                s = tti % 2
                r0 = t0 + tti * 128
                if tti >= 2:
                    dma("x%d" % s, XS[:, s, :], x_d[r0:r0 + 128, :], [], [bXS[s]])
                c0 = tti * 16
                bsm = bSMA[tti]
                S.op("dve", (lambda s_, c_: lambda e: e.bn_stats(out=SMALL[:, c_:c_ + 6], in_=XS[:, s_, 0:512]))(s, c0), [bXS[s]], [bsm])
                S.op("dve", (lambda s_, c_: lambda e: e.bn_stats(out=SMALL[:, c_ + 6:c_ + 12], in_=XS[:, s_, 512:1024]))(s, c0), [bXS[s]], [bsm])
                S.op("dve", (lambda c_: lambda e: e.bn_aggr(out=SMALL[:, c_ + 12:c_ + 14], in_=SMALL[:, c_:c_ + 12]))(c0), [bsm], [bsm])
                act(SMALL[:, c0 + 14:c0 + 15], SMALL[:, c0 + 13:c0 + 14], AF.Sqrt, [bsm, bPAR], [bsm], bias=pc("eps", 0))
                S.op("dve", (lambda c_: lambda e: e.reciprocal(out=SMALL[:, c_ + 14:c_ + 15], in_=SMALL[:, c_ + 14:c_ + 15]))(c0), [bsm], [bsm])
                stt(SMALL[:, c0 + 15:c0 + 16], SMALL[:, c0 + 12:c0 + 13], -1.0, SMALL[:, c0 + 14:c0 + 15], ALU.mult, ALU.mult, [bsm], [bsm])
                act(XS[:, s, :], XS[:, s, :], AF.Identity, [bXS[s], bsm], [bXS[s]], bias=SMALL[:, c0 + 15:c0 + 16], scale=SMALL[:, c0 + 14:c0 + 15])
                for fn_ in deferred_cp:
                    fn_()
                deferred_cp = []
                if stop == 'A1':
                    continue
                for half in range(2):
                    pt_, bpt = next_ps()
                    for q in range(4):
                        f = half * 4 + q
                        tr(pt_[:, q * 128:(q + 1) * 128], XS[:, s, f * 128:(f + 1) * 128], IDF[:], [bXS[s], bID], [bpt])
                    if stop == 'A2':
                        continue
                    for q in range(4):
                        f = half * 4 + q
                        act(HT32[:, f, tti * 128:(tti + 1) * 128], pt_[:, q * 128:(q + 1) * 128], AF.Identity,
                            [bpt, bPAR], [bHT[f]], bias=pc("ln0_b", f), scale=pc("ln0_g", f))
                        deferred_cp.append((lambda f_, t_: lambda: cp(BB[:, f_, t_ * 128:(t_ + 1) * 128], HT32[:, f_, t_ * 128:(t_ + 1) * 128], [bHT[f_]], [bBB[f_]]))(f, tti))
            for fn_ in deferred_cp:
                fn_()
            HB = BB
            bHB = bBB
            if blk + 1 < nblk:
                for s_ in range(2):
                    rn_ = (blk + 1) * TB + s_ * 128
                    dma("x%d" % s_, XS[:, s_, :], x_d[rn_:rn_ + 128, :], [], [bXS[s_]])

            if stop in ('A', 'A1', 'A2'):
                break
            dma("pos", POSI[:], pos_d[:, t0:t0 + TB].partition_broadcast(128), [], [bPOSI])
            cp(RT[:, 0, :], POSI[:], [bPOSI], [bRT[0]])
            ts(RT[:, 0, :], RT[:, 0, :], pc("invf", 0), None, ALU.mult, None, [bRT[0], bPAR], [bRT[0]])
            for which in range(2):
                if which == 0:
                    ts(RT[:, 1, :], RT[:, 0, :], pc("halfpi", 0), None, ALU.add, None, [bRT[0], bPAR], [bRT[1]])
                    src = RT[:, 1, :]
                    bsrc = bRT[1]
                else:
                    src = RT[:, 0, :]
                    bsrc = bRT[0]
                ts(POSI[:], src, float(1 / (2 * np.pi)), None, ALU.mult, None, [bsrc], [bPOSI])
                cp(CS[:, which, :], POSI[:], [bPOSI], [bCS])
                stt(src, CS[:, which, :], -C1, src, ALU.mult, ALU.add, [bCS, bsrc], [bsrc])
                stt(src, CS[:, which, :], -C2, src, ALU.mult, ALU.add, [bCS, bsrc], [bsrc])
                S.op("dve", (lambda o, i: lambda e: e.tensor_single_scalar(out=o, in_=i, scalar=float(np.pi), op=ALU.is_gt))(CS[:, which, :], src),
                     [bsrc], [bCS])
                stt(src, CS[:, which, :], float(-2 * np.pi), src, ALU.mult, ALU.add, [bCS, bsrc], [bsrc])
                S.op("dve", (lambda o, i: lambda e: e.tensor_single_scalar(out=o, in_=i, scalar=float(-np.pi), op=ALU.is_lt))(CS[:, which, :], src),
                     [bsrc], [bCS])
                stt(src, CS[:, which, :], float(2 * np.pi), src, ALU.mult, ALU.add, [bCS, bsrc], [bsrc])
                act(CS[:, which, :], src, AF.Sin, [bsrc], [bCS])
            COS = CS[:, 0, :]
            SIN = CS[:, 1, :]

            def proj_qkv(h):
                par = h % 2
                QT, KT_, VV = QTs[par], KTs[par], VVs[par]
                bQT, bKT, bVV = bQTs[par], bKTs[par], bVVs[par]
                Ws = [wload("w_in", 16 + 2 * h + c2) for c2 in range(2)]
                for tti in range(4):
                    pb, bpb = next_ps()
                    for c2 in range(2):
                        W, bW = Ws[c2]
                        for kt in range(8):
                            mm(pb[:, c2 * 256:(c2 + 1) * 256], HB[:, kt, tti * 128:(tti + 1) * 128], W[:, kt, :], kt == 0, kt == 7,
                               [bW, bHB[kt]], [bpb])
                    act(VV[:, tti, :], pb[:], AF.Copy, [bpb], [bVV])
                    if tti % 2 == 1:
                        yield
                for which, (dst, bdst, cbase) in enumerate(((QT, bQT, 8 + h), (KT_, bKT, 12 + h))):
                    W, bW = wload("w_in", cbase)
                    pa, bpa = next_ps()
                    pbb, bpbb = next_ps()
                    for kt in range(8):
                        mm(pa[:], W[:, kt, 0:128], HB[:, kt, :], kt == 0, kt == 7, [bW, bHB[kt]], [bpa])
                    for kt in range(8):
                        mm(pbb[:], W[:, kt, 128:256], HB[:, kt, :], kt == 0, kt == 7, [bW, bHB[kt]], [bpbb])
                    tt(RT[:, 0, :], pa[:], COS, ALU.mult, [bpa, bCS], [bRT[0]])
                    tt(RT[:, 1, :], pbb[:], SIN, ALU.mult, [bpbb, bCS], [bRT[1]])
                    tt(dst[:, 0, :], RT[:, 0, :], RT[:, 1, :], ALU.subtract, [bRT[0], bRT[1]], [bdst])
                    yield
                    tt(RT[:, 0, :], pbb[:], COS, ALU.mult, [bpbb, bCS], [bRT[0]])
                    tt(RT[:, 1, :], pa[:], SIN, ALU.mult, [bpa, bCS], [bRT[1]])
                    tt(dst[:, 1, :], RT[:, 0, :], RT[:, 1, :], ALU.add, [bRT[0], bRT[1]], [bdst])
                    yield

            def proj_g_kz(h):
                par = h % 2
                KT_, bKT = KTs[par], bKTs[par]
                Ws = [wload("w_in", 24 + 2 * h + c2) for c2 in range(2)]
                for tti in range(4):
                    pb, bpb = next_ps()
                    for c2 in range(2):
                        W, bW = Ws[c2]
                        for kt in range(8):
                            mm(pb[:, c2 * 256:(c2 + 1) * 256], HB[:, kt, tti * 128:(tti + 1) * 128], W[:, kt, :], kt == 0, kt == 7,
                               [bW, bHB[kt]], [bpb])
                    act(FC[:, tti, :], pb[:], AF.Silu, [bpb], [bFC[tti]])
                for tti in range(4):
                    for d in range(2):
                        tr(PSB[:, (tti * 2 + d) * 128:(tti * 2 + d + 1) * 128], KT_[:, d, tti * 128:(tti + 1) * 128], IDB[:],
                           [bKT, bID], [bPSB])
                for tti in range(4):
                    act(KZ[:, tti, :], PSB[:, tti * 256:(tti + 1) * 256], AF.Copy, [bPSB, bPAR], [bKZ], scale=pc("zs", h))

            def ret_loop(h, filler):
                par = h % 2
                QT, KT_, VV = QTs[par], KTs[par], VVs[par]
                bQT, bKT, bVV = bQTs[par], bKTs[par], bVVs[par]
                pys = {}

                def g1(tti):
                    c0_ = 64 + tti * 16
                    bsm = bSMR[tti]
                    py, bpy = pys[tti]
                    S.op("dve", (lambda p_, c_: lambda e: e.bn_stats(out=SMALL[:, c_:c_ + 6], in_=p_[:]))(py, c0_), [bpy], [bsm])
                    S.op("dve", (lambda c_: lambda e: e.bn_aggr(out=SMALL[:, c_ + 6:c_ + 8], in_=SMALL[:, c_:c_ + 6]))(c0_), [bsm], [bsm])
                    act(SMALL[:, c0_ + 8:c0_ + 9], SMALL[:, c0_ + 7:c0_ + 8], AF.Sqrt, [bsm, bPAR], [bsm], bias=pc("epsx", h))

                def g2(tti):
                    c0_ = 64 + tti * 16
                    bsm = bSMR[tti]
                    py, bpy = pys[tti]
                    rn = tti % 2
                    S.op("dve", (lambda c_: lambda e: e.reciprocal(out=SMALL[:, c_ + 8:c_ + 9], in_=SMALL[:, c_ + 8:c_ + 9]))(c0_), [bsm], [bsm])
                    stt(SMALL[:, c0_ + 9:c0_ + 10], SMALL[:, c0_ + 6:c0_ + 7], -1.0, SMALL[:, c0_ + 8:c0_ + 9], ALU.mult, ALU.mult, [bsm], [bsm])
                    act(RN[:, rn, :], py[:], AF.Identity, [bpy, bsm], [bRN[rn]], bias=SMALL[:, c0_ + 9:c0_ + 10], scale=SMALL[:, c0_ + 8:c0_ + 9])

                def g3(tti):
                    rn = tti % 2
                    tt(RB_[:, tti, :], RN[:, rn, :], FC[:, tti, :], ALU.mult, [bRN[rn], bFC[tti]], [bRB])

                for tti in range(4):
                    tsl = slice(tti * 128, (tti + 1) * 128)
                    psc, bpsc = next_ps()
                    for d in range(2):
                        mm(psc[:, 0:128], KT_[:, d, tsl], QT[:, d, tsl], d == 0, d == 1, [bKT, bQT], [bpsc])
                    tt(SD[:, tti, :], psc[:, 0:128], MK[:, h, :], ALU.mult, [bpsc, bMK], [bSD])
                    py, bpy = py_ps(tti)
                    pys[tti] = (py, bpy)
                    mm(py[:], SD[:, tti, :], VV[:, tti, :], True, False, [bSD, bVV], [bpy])
                    for d in range(2):
                        mm(py[:], QT[:, d, tsl], SBF[:, h, d, :], False, d == 1, [bQT, bSB[h][d]], [bpy])
                    for d in range(2):
                        pk, bpk = next_ps()
                        mm(pk[:], KZ[:, tti, d * 128:(d + 1) * 128], VV[:, tti, :], True, True, [bKZ, bVV], [bpk])
                        stt(SF[:, h, d, :], SF[:, h, d, :], GC[h], pk[:], ALU.mult, ALU.add, [bSF[h][d], bpk], [bSF[h][d]])
                        act(SBF[:, h, d, :], SF[:, h, d, :], AF.Copy, [bSF[h][d]], [bSB[h][d]])
                    g1(tti)
                    if tti >= 1:
                        g2(tti - 1)
                    if tti >= 2:
                        g3(tti - 2)
                    if filler is not None:
                        for _ in range(3 if tti == 0 else 1):
                            next(filler, None)
                g2(3)
                g3(2)
                g3(3)
                for e4 in range(4):
                    for tti in range(4):
                        tr(PSB[:, tti * 128:(tti + 1) * 128], RB_[:, tti, e4 * 128:(e4 + 1) * 128], IDB[:], [bRB, bID], [bPSB])
                    act(RTT[:, h * 4 + e4, :], PSB[:, 0:512], AF.Copy, [bPSB], [bRTT[h * 4 + e4]])

            if first:
                S.op("dve", lambda e: e.memset(UH[:], 0.0), [], [bUH])
                for h in range(H):
                    for d in range(2):
                        S.op("dve", (lambda h_, d_: lambda e: e.memset(SF[:, h_, d_, :], 0.0))(h, d), [], [bSF[h][d]])
                        S.op("dve", (lambda h_, d_: lambda e: e.memset(SBF[:, h_, d_, :], 0.0))(h, d), [], [bSB[h][d]])
            for f in range(8):
                cp(U[:, f, 0:30], UH[:, f, :], [bUH], [bU[f]])
            for half in range(2):
                for cpair in range(2):
                    W, bW = wload("w_in", 4 + half * 2 + cpair)
                    for q in range(2):
                        fi = cpair * 2 + q
                        pb, bpb = next_ps()
                        for kt in range(8):
                            mm(pb[:], W[:, kt, q * 128:(q + 1) * 128], HB[:, kt, :], kt == 0, kt == 7, [bW, bHB[kt]], [bpb])
                        act(FC[:, fi, :], pb[:], AF.Sigmoid, [bpb], [bFC[fi]])
                for cpair in range(2):
                    W, bW = wload("w_in", half * 2 + cpair)
                    for q in range(2):
                        fi = cpair * 2 + q
                        f = half * 4 + fi
                        pb, bpb = next_ps()
                        for kt in range(8):
                            mm(pb[:], W[:, kt, q * 128:(q + 1) * 128], HB[:, kt, :], kt == 0, kt == 7, [bW, bHB[kt]], [bpb])
                        tt(U[:, f, 30:542], pb[:], FC[:, fi, :], ALU.mult, [bpb, bFC[fi]], [bU[f]])
            o_cdw, _ = PCOLS["cdw"]
            for f in range(8):
                g = f % 2
                S.op("dve", (lambda g_, f_: lambda e: e.tensor_tensor(
                    out=DG[:, g_, :, :], in0=IDB[:].unsqueeze(1).broadcast_to([128, 31, 128]),
                    in1=PAR[:, o_cdw + f_ * 31:o_cdw + (f_ + 1) * 31].unsqueeze(2).broadcast_to([128, 31, 128]), op=ALU.mult))(g, f),
                    [bID, bPAR], [bDG[g]])
                pb, bpb = next_ps()
                for kk in range(31):
                    mm(pb[:], DG[:, g, kk, :], U[:, f, kk:kk + 512], kk == 0, kk == 30, [bDG[g], bU[f]], [bpb])
                act(FA[:, f, :], pb[:], AF.Identity, [bpb, bPAR], [bFA[f]], bias=pc("cdb", f))
                act(BA[:, f, :], pb[:], AF.Square, [bpb, bPAR], [bBA[f]], bias=pc("cdb", f))
                act(RTT[:, f, :], pb[:], AF.Identity, [bpb, bPAR], [bRTT[f]], bias=pc("cdb", f))
            for f in range(8):
                cp(UH[:, f, :], U[:, f, 512:542], [bU[f]], [bUH])
            alias_barrier(bU, hid_groups["head"])
            ln_stats(RTT, bRTT, BA, bBA)
            g0 = proj_qkv(0)
            next(g0, None)
            next(g0, None)
            for f in range(8):
                ln_norm(FA, bFA, f)
                act(BA[:, f, :], FA[:, f, :], AF.Silu, [bFA[f], bPAR], [bBA[f]], bias=pc("clb", f), scale=pc("clg", f))
            for _ in g0:
                pass
            for mp in range(4):
                Wg, bWg = wload("w_in", 32 + mp)
                Wc, bWc = wload("w_conv_o", mp)
                for q in range(2):
                    m = mp * 2 + q
                    pg, bpg = next_ps()
                    for kt in range(8):
                        mm(pg[:], Wg[:, kt, q * 128:(q + 1) * 128], HB[:, kt, :], kt == 0, kt == 7, [bWg, bHB[kt]], [bpg])
                    act(FC[:, q, :], pg[:], AF.Sigmoid, [bpg, bPAR], [bFC[q]], bias=pc("bg0", m))
                    pb, bpb = next_ps()
                    for kt in range(8):
                        mm(pb[:], Wc[:, kt, q * 128:(q + 1) * 128], BA[:, kt, :], kt == 0, kt == 7, [bWc, bBA[kt]], [bpb])
                    stt(FA[:, m, :], pb[:], pc("bco", m), FC[:, q, :], ALU.add, ALU.mult, [bpb, bPAR, bFC[q]], [bFA[m]])

            if stop == 'B':
                break
            proj_g_kz(0)
            ps_n[0] = 5
            for h in range(H):
                filler = proj_qkv(h + 1) if h + 1 < H else None
                ret_loop(h, filler)
                if filler is not None:
                    for _ in filler:
                        pass
                    proj_g_kz(h + 1)
            ps_n[0] = 7
            for mp in range(4):
                Wg, bWg = wload("w_in", 36 + mp)
                for q in range(2):
                    m = mp * 2 + q
                    Wr, bWr = wload("w_ret_o", m)
                    pg, bpg = next_ps()
                    for kt in range(8):
                        mm(pg[:], Wg[:, kt, q * 128:(q + 1) * 128], HB[:, kt, :], kt == 0, kt == 7, [bWg, bHB[kt]], [bpg])
                    act(FC[:, q, :], pg[:], AF.Sigmoid, [bpg, bPAR], [bFC[q]], bias=pc("bg1", m))
                    pb, bpb = next_ps()
                    for kt in range(16):
                        mm(pb[:], Wr[:, kt, :], RTT[:, kt, :], kt == 0, kt == 15, [bWr, bRTT[kt]], [bpb])
                    tt(FC[:, 2 + q, :], pb[:], FC[:, q, :], ALU.mult, [bpb, bFC[q]], [bFC[2 + q]])
                    tt(BA[:, m, :], FC[:, 2 + q, :], FA[:, m, :], ALU.add, [bFC[2 + q], bFA[m]], [bBA[m]])

            if stop == 'C':
                break
            for mp in range(4):
                W, bW = wload("w_out", mp)
                for q in range(2):
                    m = mp * 2 + q
                    pb, bpb = next_ps()
                    for kt in range(8):
                        mm(pb[:], W[:, kt, q * 128:(q + 1) * 128], BA[:, kt, :], kt == 0, kt == 7, [bW, bBA[kt]], [bpb])
                    stt(HT32[:, m, :], HT32[:, m, :], float(ALPHA), pb[:], ALU.mult, ALU.add, [bHT[m], bpb], [bHT[m]])
                    act(BB[:, m, :], HT32[:, m, :], AF.Square, [bHT[m]], [bBB[m]])
                    act(RTT[:, m, :], HT32[:, m, :], AF.Copy, [bHT[m]], [bRTT[m]])
            ln_stats(RTT, bRTT, BB, bBB)
            for f in range(8):
                ln_norm(HT32, bHT, f)
                act(BA[:, f, :], HT32[:, f, :], AF.Identity, [bHT[f], bPAR], [bBA[f]], bias=pc("ln1_b", f), scale=pc("ln1_g", f))
                act(HT32[:, f, :], HT32[:, f, :], AF.Identity, [bHT[f], bPAR], [bHT[f]], bias=pc("ln1_b", f), scale=pc("ln1_g", f))
            H1 = BA
            bH1 = bBA

            if stop == 'D':
                break
            alias_barrier(hid_groups["head"], bHID)
            o_fdw, _ = PCOLS["fdw"]
            o_fdb, _ = PCOLS["fdb"]
            if first:
                S.op("dve", lambda e: e.memset(ZHT[:], 0.0), [], bZH)
            ZH = ZHT[:, :].rearrange("p (c t) -> p c t", c=44)
            deferred_halo = []
            deferred_mult = []
            for jp in range(11):
                Wg, bWg = wload("w_ffn_up", jp)
                Wv, bWv = wload("w_ffn_up", 11 + jp)
                for q in range(2):
                    j = jp * 2 + q
                    accs = []
                    for part, (W, bW) in enumerate(((Wg, bWg), (Wv, bWv))):
                        c = j if part == 0 else 22 + j
                        pb, bpb = next_ps()
                        for kt in range(8):
                            mm(pb[:], W[:, kt, q * 128:(q + 1) * 128], H1[:, kt, :], kt == 0, kt == 7, [bW, bH1[kt]], [bpb])
                        aslot = (j % 2) * 3 + part
                        A_ = FA[:, aslot, :]
                        w0 = PAR[:, o_fdw + c * 3 + 0:o_fdw + c * 3 + 1]
                        w1 = PAR[:, o_fdw + c * 3 + 1:o_fdw + c * 3 + 2]
                        w2 = PAR[:, o_fdw + c * 3 + 2:o_fdw + c * 3 + 3]
                        bb_ = PAR[:, o_fdb + c:o_fdb + c + 1]
                        act(A_, pb[:], AF.Identity, [bpb, bPAR], [bFA[aslot]], bias=bb_, scale=w2)
                        act(A_[:, 0:1], ZH[:, c, 1:2], AF.Identity, [bZH[c], bPAR, bFA[aslot]], [bFA[aslot]], bias=A_[:, 0:1], scale=w1)
                        for fn_ in deferred_halo:
                            fn_()
                        deferred_halo = []
                        stt(A_[:, 1:512], pb[:, 0:511], w1, A_[:, 1:512], ALU.mult, ALU.add, [bpb, bPAR, bFA[aslot]], [bFA[aslot]])
                        stt(A_[:, 2:512], pb[:, 0:510], w0, A_[:, 2:512], ALU.mult, ALU.add, [bpb, bPAR, bFA[aslot]], [bFA[aslot]])
                        stt(A_[:, 0:2], ZH[:, c, 0:2], w0, A_[:, 0:2], ALU.mult, ALU.add, [bZH[c], bPAR, bFA[aslot]], [bFA[aslot]])
                        deferred_halo.append((lambda c_, pb_, bpb_: lambda: act(ZH[:, c_, 0:2], pb_[:, 510:512], AF.Copy, [bpb_], [bZH[c_]]))(c, pb, bpb))
                        accs.append((A_, bFA[aslot]))
                    (Ag, bAg), (Av, bAv) = accs
                    sslot = (j % 2) * 3 + 2
                    act(FA[:, sslot, :], Ag, AF.Silu, [bAg], [bFA[sslot]])
                    for fn_ in deferred_mult:
                        fn_()
                    deferred_mult = [(lambda j_, ss_, Av_, bAv_: lambda: tt(HIDV[:, j_, :], FA[:, ss_, :], Av_, ALU.mult, [bFA[ss_], bAv_], [bHID[j_]]))(j, sslot, Av, bAv)]
            for fn_ in deferred_halo + deferred_mult:
                fn_()

            dma("pld", PL[:], p_d[t0:t0 + TB, :].rearrange("(t p) c -> p t c", p=128), [], [bPL])
            for kt in range(2):
                pt_, bpt = next_ps()
                for tti in range(4):
                    tr(pt_[:, tti * 128:(tti + 1) * 128], PL[:, tti, kt * 128:(kt + 1) * 128], IDF[:], [bPL, bID], [bpt])
                act(PT[:, kt, :], pt_[:], AF.Copy, [bpt], [bPT])
            for mp in range(4):
                Wg, bWg = wload("w_ple_gate", mp)
                Wp, bWp = wload("w_ple", mp)
                for q in range(2):
                    m = mp * 2 + q
                    pg, bpg = next_ps()
                    for kt in range(8):
                        mm(pg[:], Wg[:, kt, q * 128:(q + 1) * 128], H1[:, kt, :], kt == 0, kt == 7, [bWg, bH1[kt]], [bpg])
                    act(FC[:, q, :], pg[:], AF.Sigmoid, [bpg, bPAR], [bFC[q]], bias=pc("bpg", m))
                    pp, bpp = next_ps()
                    for kt in range(2):
                        mm(pp[:], Wp[:, kt, q * 128:(q + 1) * 128], PT[:, kt, :], kt == 0, kt == 1, [bWp, bPT], [bpp])
                    tt(FC[:, 2 + q, :], pp[:], FC[:, q, :], ALU.mult, [bpp, bFC[q]], [bFC[2 + q]])
                    pf, bpf = next_ps()
                    for half in range(2):
                        Wd, bWd = wload("w_ffn_down", m * 2 + half)
                        for kt in range(11):
                            j = half * 11 + kt
                            mm(pf[:], Wd[:, kt, :], HIDV[:, j, :], j == 0, j == 21, [bWd, bHID[j]], [bpf])
                    stt(HT32[:, m, :], HT32[:, m, :], float(ALPHA), pf[:], ALU.mult, ALU.add, [bHT[m], bpf], [bHT[m]])
                    tt(HT32[:, m, :], HT32[:, m, :], FC[:, 2 + q, :], ALU.add, [bHT[m], bFC[2 + q]], [bHT[m]])
                    act(BB[:, m, :], HT32[:, m, :], AF.Square, [bHT[m]], [bBB[m]])
                    act(RTT[:, m, :], HT32[:, m, :], AF.Copy, [bHT[m]], [bRTT[m]])
            ln_stats(RTT, bRTT, BB, bBB)
            for f in range(8):
                ln_norm(HT32, bHT, f)
                act(HT32[:, f, :], HT32[:, f, :], AF.Identity, [bHT[f], bPAR], [bHT[f]], bias=pc("ln2_b", f), scale=pc("ln2_g", f))
            for tti in range(4):
                s = tti % 2
                for half in range(2):
                    pt_, bpt = next_ps()
                    for q in range(4):
                        f = half * 4 + q
                        tr(pt_[:, q * 128:(q + 1) * 128], HT32[:, f, tti * 128:(tti + 1) * 128], IDF[:], [bHT[f], bID], [bpt])
                    if half == 0:
                        act(OS[:, s, 0:512], pt_[:], AF.Copy, [bpt], [bOS[s]])
                    else:
                        cp(OS[:, s, 512:1024], pt_[:], [bpt], [bOS[s]])
                r0 = t0 + tti * 128
                out_toks[s] = dma("o%d" % s, out_d[r0:r0 + 128, :], OS[:, s, :], [bOS[s]], [bOUT])
            alias_barrier(bHID, bU)
            alias_barrier(bHID, hid_groups["head"])

        S.wait_all("sp", list(out_toks.values()))
        S.emit()
    return nc


_CACHE = {}


def kernel(**inp):
    inp = {k: np.asarray(v) for k, v in inp.items()}
    P, mk = _pack_params(inp)
    packs = {
        "w_in": _pack_w(np.asarray(inp["w_in"][0], np.float32), 8, 256),
        "w_conv_o": _pack_w(np.asarray(inp["w_conv_o"][0], np.float32), 8, 256),
        "w_out": _pack_w(np.asarray(inp["w_out"][0], np.float32), 8, 256),
        "w_ple_gate": _pack_w(np.asarray(inp["w_ple_gate"][0], np.float32), 8, 256),
        "w_ffn_up": _pack_w(np.asarray(inp["w_ffn_up"][0], np.float32), 8, 256),
        "w_ret_o": _pack_w(np.asarray(inp["w_ret_o"][0], np.float32), 16, 128),
        "w_ffn_down": _pack_wdown(np.asarray(inp["w_ffn_down"][0], np.float32)),
        "w_ple": _pack_w(np.asarray(inp["w_ple"][0], np.float32), 2, 256),
    }
    ident = np.eye(128, dtype=np.float32)
    x = np.asarray(inp["x"], np.float32)
    pos = np.asarray(inp["positions"], np.int32)
    pp = np.asarray(inp["p"][0], np.float32)
    if "nc" not in _CACHE:
        _CACHE["nc"] = build_nc()
    nc = _CACHE["nc"]
    in_maps = []
    for c in range(NCORES):
        m = {
            "x": np.ascontiguousarray(x[2 * c:2 * c + 2].reshape(TOK, D)),
            "pos": np.ascontiguousarray(pos[2 * c:2 * c + 2].reshape(1, TOK)),
            "p": np.ascontiguousarray(pp[2 * c:2 * c + 2].reshape(TOK, 256)),
            "params": P,
            "mask": np.ascontiguousarray(mk.reshape(128, H * 128)),
            "ident": ident,
        }
        m.update(packs)
        in_maps.append(m)
    res = run_bass_kernel_spmd(nc, in_maps, core_ids=list(range(NCORES)))
    out = np.stack([np.asarray(r["out"], np.float32).reshape(2, SEQ, D) for r in res.results], axis=0)
    return out.reshape(16, SEQ, D)
```

```python
import contextlib
import numpy as np
import concourse.bass as bass
import concourse.mybir as mybir
from concourse.bass_utils import run_bass_kernel_spmd

F32 = mybir.dt.float32
BF16 = mybir.dt.bfloat16
I32 = mybir.dt.int32
AF = mybir.ActivationFunctionType
ALU = mybir.AluOpType

NCORES = 8
D = 1024
SEQ = 2048
TB = 512
NBLK = 8
TOK = 4096
H = 4
FFN = 2816
NJ = 22
EPS = 1e-5
ALPHA = 2.0 ** 0.25
COMPUTE = ("pe", "act", "dve", "pool")


class Buf:
    __slots__ = ("name", "last_w", "readers", "excl")

    def __init__(self, name, excl=False):
        self.name = name
        self.last_w = None
        self.readers = []
        self.excl = excl


def alias_barrier(old_bufs, new_bufs):
    toks = []
    for b in old_bufs:
        if b.last_w is not None:
            toks.append(b.last_w)
        toks.extend(b.readers)
    for nb in new_bufs:
        nb.readers.extend(toks)


class Sched:
    def __init__(self, nc):
        self.nc = nc
        self.ops = {e: [] for e in ("pe", "act", "dve", "pool", "sp")}
        self.dma_count = {}
        self.known = {e: {} for e in self.ops}

    def _waits(self, eng, reads, writes, is_dma):
        deps = []
        for b in reads:
            if b.last_w is not None:
                deps.append(b.last_w)
        for b in writes:
            if b.last_w is not None:
                deps.append(b.last_w)
            for r in b.readers:
                deps.append(r)
        out = {}
        for (key, val, kind) in deps:
            if (not is_dma) and kind == "c" and key == eng and eng == "pe":
                continue
            k = (key, kind)
            if k not in out or out[k] < val:
                out[k] = val
        waits = []
        kn = self.known[eng]
        for (key, kind), val in out.items():
            if kn.get((key, kind), -1) >= val:
                continue
            kn[(key, kind)] = val
            waits.append((key, val, kind))
        return waits

    @staticmethod
    def _finish(tok, reads, writes):
        for b in reads:
            b.readers.append(tok)
        for b in writes:
            b.last_w = tok
            b.readers = []

    def op(self, eng, fn, reads=(), writes=()):
        reads = [b for b in reads if b is not None]
        writes = [b for b in writes if b is not None]
        writes = writes + [b for b in reads if b.excl and b not in writes]
        waits = self._waits(eng, reads, writes, False)
        tok = (eng, len(self.ops[eng]), "c")
        self.ops[eng].append({"fn": fn, "waits": waits, "inc": None, "need": False})
        self._finish(tok, reads, writes)
        return tok

    def dma(self, q, sem, fn, reads=(), writes=()):
        reads = [b for b in reads if b is not None]
        writes = [b for b in writes if b is not None]
        waits = self._waits(q, reads, writes, True)
        n = self.dma_count.get(sem, 0) + 1
        self.dma_count[sem] = n
        tok = (sem, 16 * n, "d")
        self.ops[q].append({"fn": fn, "waits": waits, "inc": (sem, 16), "need": True})
        self._finish(tok, reads, writes)
        return tok

    def wait_all(self, eng, toks):
        self.ops[eng].append({"fn": None, "waits": list(toks), "inc": None, "need": False})

    def emit(self):
        nc = self.nc
        for e, lst in self.ops.items():
            for o in lst:
                for (key, val, kind) in o["waits"]:
                    if kind == "c":
                        self.ops[key][val]["need"] = True
        NS = 8
        cum = {}
        for e in COMPUTE:
            c = [0] * NS
            arr = []
            for i, o in enumerate(self.ops[e]):
                if o["need"] and o["inc"] is None:
                    c[i % NS] += 1
                arr.append(c[i % NS])
            cum[e] = arr
        with contextlib.ExitStack() as st:
            sems = {}
            for e in COMPUTE:
                if any(o["need"] and o["inc"] is None for o in self.ops[e]):
                    sems[e] = [st.enter_context(nc.semaphore("s_%s%d" % (e, i))) for i in range(NS)]
            for s in self.dma_count:
                sems[s] = st.enter_context(nc.semaphore("d_" + s))
            block = st.enter_context(nc.Block())

            def run(engname):
                def body(eng):
                    for i, o in enumerate(self.ops[engname]):
                        for (key, val, kind) in o["waits"]:
                            if kind == "c":
                                eng.wait_ge(sems[key][val % NS], cum[key][val])
                            else:
                                eng.wait_ge(sems[key], val)
                        if o["fn"] is None:
                            continue
                        ins = o["fn"](eng)
                        if o["inc"] is not None:
                            ins.then_inc(sems[o["inc"][0]], o["inc"][1])
                        elif o["need"]:
                            ins.then_inc(sems[engname][i % NS], 1)
                return body

            block.tensor(run("pe"))
            block.scalar(run("act"))
            block.vector(run("dve"))
            block.gpsimd(run("pool"))
            block.sync(run("sp"))


WSPEC = {
    "w_in": (8, 256, 40),
    "w_conv_o": (8, 256, 4),
    "w_out": (8, 256, 4),
    "w_ple_gate": (8, 256, 4),
    "w_ffn_up": (8, 256, 22),
    "w_ret_o": (16, 128, 8),
    "w_ffn_down": (11, 128, 16),
    "w_ple": (2, 256, 4),
}


def _pack_w(w, kt, c):
    K, N = w.shape
    nch = N // c
    a = w.reshape(kt, 128, nch, c)
    return np.ascontiguousarray(a.transpose(2, 1, 0, 3)).reshape(nch, 128, kt * c)


def _pack_wdown(w):
    a = w.reshape(2, 11, 128, 8, 128)
    return np.ascontiguousarray(a.transpose(3, 0, 2, 1, 4)).reshape(16, 128, 11 * 128)


PCOLS = {}


def _pcol(name, n, _state=[0]):
    PCOLS[name] = (_state[0], n)
    _state[0] += n


for _n, _w in [("ln0_g", 8), ("ln0_b", 8), ("bg0", 8), ("bg1", 8), ("cdw", 8 * 31), ("cdb", 8), ("clg", 8), ("clb", 8),
               ("bco", 8), ("gng", 16), ("ln1_g", 8), ("ln1_b", 8), ("fdw", 44 * 3), ("fdb", 44), ("bpg", 8),
               ("ln2_g", 8), ("ln2_b", 8), ("zs", 4), ("epsx", 4), ("invf", 1), ("eps", 1), ("halfpi", 1)]:
    _pcol(_n, _w)
NP = sum(v[1] for v in PCOLS.values())


def _vec(v):
    return np.ascontiguousarray(np.asarray(v, np.float32).reshape(-1, 128).T)


def _consts():
    gam = 1.0 - 2.0 ** (-5.0 - np.arange(H, dtype=np.float64))
    lg = np.log(gam)
    idx = np.arange(128, dtype=np.float64)
    i = idx[None, :]
    j = idx[:, None]
    ci = (i // 64)
    cj = (j // 64)
    mk = np.zeros((128, H, 128), np.float64)
    for h in range(H):
        w = np.where(ci == cj, np.exp(lg[h] * np.abs(i - j)), np.where(ci > cj, np.exp(lg[h] * (i - j)), 0.0))
        w = w * np.exp(-lg[h] * (i + 1.0)) / 16.0
        mk[:, h, :] = w
    zs = np.exp(lg[None, :] * (127.0 - idx[:, None])) / 16.0
    epsx = EPS * np.exp(-2.0 * lg[None, :] * (idx[:, None] + 1.0))
    gc = np.exp(lg * 128.0)
    invf = 10000.0 ** (-np.arange(128, dtype=np.float32) / np.float32(128))
    return mk.astype(np.float32), zs.astype(np.float32), epsx.astype(np.float32), [float(g) for g in gc], invf.astype(np.float32)


def _pack_params(inp):
    P = np.zeros((128, NP), np.float32)

    def put(name, arr):
        o, n = PCOLS[name]
        assert arr.shape == (128, n), (name, arr.shape, n)
        P[:, o:o + n] = arr

    put("ln0_g", _vec(inp["ln0_g"]))
    put("ln0_b", _vec(inp["ln0_b"]))
    put("bg0", _vec(inp["b_gate"][0, 0]))
    put("bg1", _vec(inp["b_gate"][0, 1]))
    cw = np.asarray(inp["conv_dw_w"][0], np.float32)
    put("cdw", np.ascontiguousarray(cw.reshape(31, 8, 128).transpose(2, 1, 0)).reshape(128, 8 * 31))
    put("cdb", _vec(inp["conv_dw_b"][0]))
    put("clg", _vec(inp["conv_ln_g"][0]))
    put("clb", _vec(inp["conv_ln_b"][0]))
    put("bco", _vec(inp["b_conv_o"][0]))
    put("gng", _vec(np.asarray(inp["ret_gn_g"][0]).reshape(-1)))
    put("ln1_g", _vec(inp["ln1_g"][0]))
    put("ln1_b", _vec(inp["ln1_b"][0]))
    fw = np.asarray(inp["ffn_dw_w"][0], np.float32)
    put("fdw", np.ascontiguousarray(fw.reshape(3, 44, 128).transpose(2, 1, 0)).reshape(128, 44 * 3))
    put("fdb", _vec(inp["ffn_dw_b"][0]))
    put("bpg", _vec(inp["b_ple_gate"][0]))
    put("ln2_g", _vec(inp["ln2_g"][0]))
    put("ln2_b", _vec(inp["ln2_b"][0]))
    mk, zs, epsx, gc, invf = _consts()
    put("zs", zs)
    put("epsx", epsx)
    put("invf", invf.reshape(128, 1))
    put("eps", np.full((128, 1), EPS, np.float32))
    put("halfpi", np.full((128, 1), np.pi / 2, np.float32))
    return P, mk


def build_nc(nblk=NBLK, stop=None):
    nc = bass.Bass("TRN2", target_bir_lowering=False)
    _, _, _, GC, _ = _consts()

    x_d = nc.dram_tensor("x", [TOK, D], F32, kind="ExternalInput").ap()
    pos_d = nc.dram_tensor("pos", [1, TOK], I32, kind="ExternalInput").ap()
    p_d = nc.dram_tensor("p", [TOK, 256], F32, kind="ExternalInput").ap()
    par_d = nc.dram_tensor("params", [128, NP], F32, kind="ExternalInput").ap()
    mk_d = nc.dram_tensor("mask", [128, H * 128], F32, kind="ExternalInput").ap()
    id_d = nc.dram_tensor("ident", [128, 128], F32, kind="ExternalInput").ap()
    wd = {}
    ws = {}
    for name, (kt, c, nch) in WSPEC.items():
        wd[name] = nc.dram_tensor(name, [nch, 128, kt * c], F32, kind="ExternalInput").ap()
        ws[name] = nc.dram_tensor(name + "_bf", [nch, 128, kt * c], BF16).ap()
    out_d = nc.dram_tensor("out", [TOK, D], F32, kind="ExternalOutput").ap()

    S = Sched(nc)
    with contextlib.ExitStack() as st:
        def sb(name, shape, dt):
            return st.enter_context(nc.sbuf_tensor(name, shape, dt))

        def ps(name, shape, dt):
            return st.enter_context(nc.psum_tensor(name, shape, dt))

        PAR = sb("PAR", [128, NP], F32)
        MK = sb("MK", [128, H, 128], F32)
        IDF = sb("IDF", [128, 128], F32)
        IDB = sb("IDB", [128, 128], BF16)
        ONF = sb("ONF", [128, 128], F32)
        ONB = sb("ONB", [128, 128], BF16)
        HT32 = sb("HT32", [128, 8, TB], F32)
        FA = sb("FA", [128, 8, TB], F32)
        FC = sb("FC", [128, 4, TB], F32)
        XS = sb("XS", [128, 2, D], F32)
        ST = sb("ST", [128, 3, TB], F32)
        CS = sb("CS", [128, 2, TB], F32)
        RT = sb("RT", [128, 2, TB], F32)
        RN = sb("RN", [128, 2, TB], F32)
        SF = sb("SF", [128, H, 2, 512], F32)
        PL = sb("PL", [128, 4, 256], F32)
        OS = sb("OS", [128, 2, D], F32)
        SMALL = sb("SMALL", [128, 128], F32)
        POSI = sb("POSI", [128, TB], I32)
        BA = sb("BA", [128, 8, TB], BF16)
        BB = sb("BB", [128, 8, TB], BF16)
        HID = sb("HID", [128, NJ * TB], BF16)
        UH = sb("UH", [128, 8, 30], BF16)
        DG = sb("DG", [128, 2, 31, 128], BF16)
        RTT = sb("RTT", [128, 16, TB], BF16)
        SBF = sb("SBF", [128, H, 2, 512], BF16)
        WR = sb("WR", [128, 4, 2048], BF16)
        PT = sb("PT", [128, 2, TB], BF16)
        SDT = sb("SDT", [128, 4, 128], BF16)
        ZHT = sb("ZHT", [128, 176], F32)
        PSF = [ps("PS%d" % i, [128, 512], F32) for i in range(7)]
        PSB = ps("PSB", [128, 1024], BF16)
        bPS = [Buf("ps%d" % i, True) for i in range(7)]
        bPSB = Buf("psb", True)
        ps_rr = [0]

        ps_n = [7]

        def next_ps():
            i = ps_rr[0] % ps_n[0]
            ps_rr[0] += 1
            return PSF[i], bPS[i]

        def py_ps(t):
            return PSF[5 + t % 2], bPS[5 + t % 2]

        def pc(name, f=None, n=1):
            o, w = PCOLS[name]
            if f is None:
                return PAR[:, o:o + w]
            return PAR[:, o + f:o + f + n]

        def mm(out, lhsT, rhs, start, stop, r, w):
            S.op("pe", lambda e: e.matmul(out, lhsT=lhsT, rhs=rhs, start=start, stop=stop), r, w)

        def tr(out, in_, ident, r, w):
            S.op("pe", lambda e: e.transpose(out, in_, ident), r, w)

        def act(out, in_, func, r, w, bias=None, scale=None):
            kw = {}
            if bias is not None:
                kw["bias"] = bias
            if scale is not None:
                kw["scale"] = scale
            S.op("act", lambda e: e.activation(out=out, in_=in_, func=func, **kw), r, w)

        def tt(out, in0, in1, op, r, w):
            S.op("dve", lambda e: e.tensor_tensor(out=out, in0=in0, in1=in1, op=op), r, w)

        def ts(out, in0, s1, s2, op0, op1, r, w):
            if s2 is None:
                S.op("dve", lambda e: e.tensor_scalar(out=out, in0=in0, scalar1=s1, scalar2=None, op0=op0), r, w)
            else:
                S.op("dve", lambda e: e.tensor_scalar(out=out, in0=in0, scalar1=s1, scalar2=s2, op0=op0, op1=op1), r, w)

        def stt(out, in0, scalar, in1, op0, op1, r, w):
            S.op("dve", lambda e: e.scalar_tensor_tensor(out=out, in0=in0, scalar=scalar, in1=in1, op0=op0, op1=op1), r, w)

        def cp(out, in_, r, w):
            S.op("dve", lambda e: e.tensor_copy(out=out, in_=in_), r, w)

        def dma(sem, out, in_, r, w):
            return S.dma("sp", sem, lambda e: e.dma_start(out=out, in_=in_), r, w)

        bPAR, bMK, bID, bON = Buf("par"), Buf("mk"), Buf("id"), Buf("on")
        bHT = [Buf("ht%d" % f) for f in range(8)]
        bFA = [Buf("fa%d" % f) for f in range(8)]
        bFC = [Buf("fc%d" % f) for f in range(4)]
        bXS = [Buf("xs%d" % i) for i in range(2)]
        bST = [Buf("st%d" % i) for i in range(3)]
        bCS, bRT, bRN = Buf("cs"), [Buf("rt0"), Buf("rt1")], [Buf("rn0"), Buf("rn1")]
        bSMA = [Buf("sma%d" % i) for i in range(4)]
        bSMR = [Buf("smr%d" % i) for i in range(4)]
        bSF = [[Buf("sf%d%d" % (h, d)) for d in range(2)] for h in range(H)]
        bSB = [[Buf("sb%d%d" % (h, d)) for d in range(2)] for h in range(H)]
        bPL, bOS, bSM, bPOSI = Buf("pl"), [Buf("os0"), Buf("os1")], Buf("small"), Buf("posi")
        bBA = [Buf("ba%d" % f) for f in range(8)]
        bBB = [Buf("bb%d" % f) for f in range(8)]
        bUH, bDG = Buf("uh"), [Buf("dg0"), Buf("dg1")]
        bRTT = [Buf("rtt%d" % i) for i in range(16)]
        bWR = [Buf("wr%d" % i) for i in range(4)]
        bPT = Buf("pt")
        bWS = {name: [Buf("ws_%s%d" % (name, i)) for i in range(WSPEC[name][2])] for name in WSPEC}
        bZH = [Buf("zh%d" % i) for i in range(44)]
        out_toks = {}
        bOUT = Buf("out")
        U = HID[:, 0:8 * 542].rearrange("p (f t) -> p f t", f=8)
        bU = [Buf("u%d" % f) for f in range(8)]
        QTs = [HID[:, 1024 * i:1024 * (i + 1)].rearrange("p (d t) -> p d t", d=2) for i in range(2)]
        KTs = [HID[:, 2048 + 1024 * i:2048 + 1024 * (i + 1)].rearrange("p (d t) -> p d t", d=2) for i in range(2)]
        KZ = HID[:, 4096:5120].rearrange("p (t d) -> p t d", t=4)
        VVs = [HID[:, 5120 + 2048 * i:5120 + 2048 * (i + 1)].rearrange("p (t e) -> p t e", t=4) for i in range(2)]
        RB_ = HID[:, 9216:11264].rearrange("p (t e) -> p t e", t=4)
        SD = SDT
        bQTs, bKTs, bVVs = [Buf("qt0"), Buf("qt1")], [Buf("kt0"), Buf("kt1")], [Buf("vv0"), Buf("vv1")]
        bKZ, bSD, bRB = Buf("kz"), Buf("sd"), Buf("rb")
        HIDV = HID[:, :].rearrange("p (j t) -> p j t", j=NJ)
        bHID = [Buf("hid%d" % j) for j in range(NJ)]
        hid_groups = {"u": bU, "head": bQTs + bKTs + bVVs + [bKZ, bRB], "hid": bHID}

        dma("c0", PAR[:], par_d, [], [bPAR])
        dma("c1", MK[:].rearrange("p h i -> p (h i)"), mk_d, [], [bMK])
        dma("c2", IDF[:], id_d, [], [bID])
        cp(IDB[:], IDF[:], [bID], [bID])
        S.op("dve", lambda e: e.memset(ONF[:], 1.0 / 1024), [], [bON])
        S.op("dve", lambda e: e.memset(ONB[:], 1.0 / 1024), [], [bON])

        stg_f = [FA[:, 0:4, :].rearrange("p a t -> p (a t)"), FA[:, 4:8, :].rearrange("p a t -> p (a t)"),
                 HT32[:, 0:4, :].rearrange("p a t -> p (a t)"), HT32[:, 4:8, :].rearrange("p a t -> p (a t)")]
        bstg_f = [Buf("stgf%d" % i) for i in range(4)]
        stg_b = [HID[:, 2048 * i:2048 * (i + 1)] for i in range(4)]
        bstg_b = [Buf("stgb%d" % i) for i in range(4)]
        order = ["w_in", "w_conv_o", "w_ret_o", "w_out", "w_ffn_up", "w_ple_gate", "w_ple", "w_ffn_down"]
        chunks = [(name, ci) for name in order for ci in range(WSPEC[name][2])]

        def pro_load(k):
            name, ci = chunks[k]
            kt, c, nch = WSPEC[name]
            s = k % 4
            dma("pl%d" % s, stg_f[s][:, 0:kt * c], wd[name][ci], [], [bstg_f[s]])

        LOOK = 3
        for k in range(min(LOOK, len(chunks))):
            pro_load(k)
        for k, (name, ci) in enumerate(chunks):
            kt, c, nch = WSPEC[name]
            n = kt * c
            s = k % 4
            if name == "w_ret_o":
                o_, _ = PCOLS["gng"]
                S.op("dve", (lambda sf, sbf: lambda e: e.tensor_tensor(
                    out=sbf.rearrange("p (k c) -> p k c", k=16), in0=sf.rearrange("p (k c) -> p k c", k=16),
                    in1=PAR[:, o_:o_ + 16].unsqueeze(2).broadcast_to([128, 16, 128]), op=ALU.mult))(stg_f[s][:, 0:n], stg_b[s][:, 0:n]),
                    [bstg_f[s], bPAR], [bstg_b[s]])
            elif k % 2 == 0:
                act(stg_b[s][:, 0:n], stg_f[s][:, 0:n], AF.Copy, [bstg_f[s]], [bstg_b[s]])
            else:
                cp(stg_b[s][:, 0:n], stg_f[s][:, 0:n], [bstg_f[s]], [bstg_b[s]])
            dma("ps%d" % s, ws[name][ci], stg_b[s][:, 0:n], [bstg_b[s]], [bWS[name][ci]])
            if k + LOOK < len(chunks):
                pro_load(k + LOOK)
        alias_barrier(bstg_f, bFA)
        alias_barrier(bstg_f, bHT)
        alias_barrier(bstg_b, bU + hid_groups["head"] + bHID)

        wr_rr = [0]

        def wload(name, ci):
            kt, c, nch = WSPEC[name]
            n = kt * c
            s = wr_rr[0] % 4
            wr_rr[0] += 1
            dma("w%d" % s, WR[:, s, 0:n], ws[name][ci], [bWS[name][ci]], [bWR[s]])
            return WR[:, s, 0:n].rearrange("p (k c) -> p k c", k=kt), bWR[s]

        def ln_stats(SRCB, bSRCB, SQ, bSQ):
            pm, bpm = next_ps()
            for f in range(8):
                mm(pm[:], ONB[:], SRCB[:, f, :], f == 0, f == 7, [bON, bSRCB[f]], [bpm])
            pe_, bpe = next_ps()
            for f in range(8):
                mm(pe_[:], ONB[:], SQ[:, f, :], f == 0, f == 7, [bON, bSQ[f]], [bpe])
            act(ST[:, 0, :], pm[:], AF.Copy, [bpm], [bST[0]])
            act(ST[:, 1, :], pm[:], AF.Square, [bpm], [bST[1]])
            tt(ST[:, 1, :], pe_[:], ST[:, 1, :], ALU.subtract, [bpe, bST[1]], [bST[1]])
            act(ST[:, 1, :], ST[:, 1, :], AF.Sqrt, [bST[1], bPAR], [bST[1]], bias=pc("eps", 0))
            S.op("dve", lambda e: e.reciprocal(out=ST[:, 1, :], in_=ST[:, 1, :]), [bST[1]], [bST[1]])

        def ln_norm(SRC, bSRC, f):
            tt(SRC[:, f, :], SRC[:, f, :], ST[:, 0, :], ALU.subtract, [bSRC[f], bST[0]], [bSRC[f]])
            tt(SRC[:, f, :], SRC[:, f, :], ST[:, 1, :], ALU.mult, [bSRC[f], bST[1]], [bSRC[f]])

        C1 = 6.28125
        C2 = 2 * np.pi - 6.28125

        for s_ in range(2):
            dma("x%d" % s_, XS[:, s_, :], x_d[s_ * 128:(s_ + 1) * 128, :], [], [bXS[s_]])
        for blk in range(nblk):
            t0 = blk * TB
            first = (blk % 4 == 0)
            if stop == 'P':
                break
            if stop == 'pos':
                break
            deferred_cp = []
            for tti in range(4):
                s = tti % 2
                r0 = t0 + tti * 128
                if tti >= 2:
                    dma("x%d" % s, XS[:, s, :], x_d[r0:r0 + 128, :], [], [bXS[s]])
                c0 = tti * 16
                bsm = bSMA[tti]
                S.op("dve", (lambda s_, c_: lambda e: e.bn_stats(out=SMALL[:, c_:c_ + 6], in_=XS[:, s_, 0:512]))(s, c0), [bXS[s]], [bsm])
                S.op("dve", (lambda s_, c_: lambda e: e.bn_stats(out=SMALL[:, c_ + 6:c_ + 12], in_=XS[:, s_, 512:1024]))(s, c0), [bXS[s]], [bsm])
                S.op("dve", (lambda c_: lambda e: e.bn_aggr(out=SMALL[:, c_ + 12:c_ + 14], in_=SMALL[:, c_:c_ + 12]))(c0), [bsm], [bsm])
                act(SMALL[:, c0 + 14:c0 + 15], SMALL[:, c0 + 13:c0 + 14], AF.Sqrt, [bsm, bPAR], [bsm], bias=pc("eps", 0))
                S.op("dve", (lambda c_: lambda e: e.reciprocal(out=SMALL[:, c_ + 14:c_ + 15], in_=SMALL[:, c_ + 14:c_ + 15]))(c0), [bsm], [bsm])
                stt(SMALL[:, c0 + 15:c0 + 16], SMALL[:, c0 + 12:c0 + 13], -1.0, SMALL[:, c0 + 14:c0 + 15], ALU.mult, ALU.mult, [bsm], [bsm])
                act(XS[:, s, :], XS[:, s, :], AF.Identity, [bXS[s], bsm], [bXS[s]], bias=SMALL[:, c0 + 15:c0 + 16], scale=SMALL[:, c0 + 14:c0 + 15])
                for fn_ in deferred_cp:
                    fn_()
                deferred_cp = []
                if stop == 'A1':
                    continue
                for half in range(2):
                    pt_, bpt = next_ps()
                    for q in range(4):
                        f = half * 4 + q
                        tr(pt_[:, q * 128:(q + 1) * 128], XS[:, s, f * 128:(f + 1) * 128], IDF[:], [bXS[s], bID], [bpt])
                    if stop == 'A2':
                        continue
                    for q in range(4):
                        f = half * 4 + q
                        act(HT32[:, f, tti * 128:(tti + 1) * 128], pt_[:, q * 128:(q + 1) * 128], AF.Identity,
                            [bpt, bPAR], [bHT[f]], bias=pc("ln0_b", f), scale=pc("ln0_g", f))
                        deferred_cp.append((lambda f_, t_: lambda: cp(BB[:, f_, t_ * 128:(t_ + 1) * 128], HT32[:, f_, t_ * 128:(t_ + 1) * 128], [bHT[f_]], [bBB[f_]]))(f, tti))
            for fn_ in deferred_cp:
                fn_()
            HB = BB
            bHB = bBB
            if blk + 1 < nblk:
                for s_ in range(2):
                    rn_ = (blk + 1) * TB + s_ * 128
                    dma("x%d" % s_, XS[:, s_, :], x_d[rn_:rn_ + 128, :], [], [bXS[s_]])

            if stop in ('A', 'A1', 'A2'):
                break
            dma("pos", POSI[:], pos_d[:, t0:t0 + TB].partition_broadcast(128), [], [bPOSI])
            cp(RT[:, 0, :], POSI[:], [bPOSI], [bRT[0]])
            ts(RT[:, 0, :], RT[:, 0, :], pc("invf", 0), None, ALU.mult, None, [bRT[0], bPAR], [bRT[0]])
            for which in range(2):
                if which == 0:
                    ts(RT[:, 1, :], RT[:, 0, :], pc("halfpi", 0), None, ALU.add, None, [bRT[0], bPAR], [bRT[1]])
                    src = RT[:, 1, :]
                    bsrc = bRT[1]
                else:
                    src = RT[:, 0, :]
                    bsrc = bRT[0]
                ts(POSI[:], src, float(1 / (2 * np.pi)), None, ALU.mult, None, [bsrc], [bPOSI])
                cp(CS[:, which, :], POSI[:], [bPOSI], [bCS])
                stt(src, CS[:, which, :], -C1, src, ALU.mult, ALU.add, [bCS, bsrc], [bsrc])
                stt(src, CS[:, which, :], -C2, src, ALU.mult, ALU.add, [bCS, bsrc], [bsrc])
                S.op("dve", (lambda o, i: lambda e: e.tensor_single_scalar(out=o, in_=i, scalar=float(np.pi), op=ALU.is_gt))(CS[:, which, :], src),
                     [bsrc], [bCS])
                stt(src, CS[:, which, :], float(-2 * np.pi), src, ALU.mult, ALU.add, [bCS, bsrc], [bsrc])
                S.op("dve", (lambda o, i: lambda e: e.tensor_single_scalar(out=o, in_=i, scalar=float(-np.pi), op=ALU.is_lt))(CS[:, which, :], src),
                     [bsrc], [bCS])
                stt(src, CS[:, which, :], float(2 * np.pi), src, ALU.mult, ALU.add, [bCS, bsrc], [bsrc])
                act(CS[:, which, :], src, AF.Sin, [bsrc], [bCS])
            COS = CS[:, 0, :]
            SIN = CS[:, 1, :]

            def proj_qkv(h):
                par = h % 2
                QT, KT_, VV = QTs[par], KTs[par], VVs[par]
                bQT, bKT, bVV = bQTs[par], bKTs[par], bVVs[par]
                Ws = [wload("w_in", 16 + 2 * h + c2) for c2 in range(2)]
                for tti in range(4):
                    pb, bpb = next_ps()
                    for c2 in range(2):
                        W, bW = Ws[c2]
                        for kt in range(8):
                            mm(pb[:, c2 * 256:(c2 + 1) * 256], HB[:, kt, tti * 128:(tti + 1) * 128], W[:, kt, :], kt == 0, kt == 7,
                               [bW, bHB[kt]], [bpb])
                    act(VV[:, tti, :], pb[:], AF.Copy, [bpb], [bVV])
                    if tti % 2 == 1:
                        yield
                for which, (dst, bdst, cbase) in enumerate(((QT, bQT, 8 + h), (KT_, bKT, 12 + h))):
                    W, bW = wload("w_in", cbase)
                    pa, bpa = next_ps()
                    pbb, bpbb = next_ps()
                    for kt in range(8):
                        mm(pa[:], W[:, kt, 0:128], HB[:, kt, :], kt == 0, kt == 7, [bW, bHB[kt]], [bpa])
                    for kt in range(8):
                        mm(pbb[:], W[:, kt, 128:256], HB[:, kt, :], kt == 0, kt == 7, [bW, bHB[kt]], [bpbb])
                    tt(RT[:, 0, :], pa[:], COS, ALU.mult, [bpa, bCS], [bRT[0]])
                    tt(RT[:, 1, :], pbb[:], SIN, ALU.mult, [bpbb, bCS], [bRT[1]])
                    tt(dst[:, 0, :], RT[:, 0, :], RT[:, 1, :], ALU.subtract, [bRT[0], bRT[1]], [bdst])
                    yield
                    tt(RT[:, 0, :], pbb[:], COS, ALU.mult, [bpbb, bCS], [bRT[0]])
                    tt(RT[:, 1, :], pa[:], SIN, ALU.mult, [bpa, bCS], [bRT[1]])
                    tt(dst[:, 1, :], RT[:, 0, :], RT[:, 1, :], ALU.add, [bRT[0], bRT[1]], [bdst])
                    yield

            def proj_g_kz(h):
                par = h % 2
                KT_, bKT = KTs[par], bKTs[par]
                Ws = [wload("w_in", 24 + 2 * h + c2) for c2 in range(2)]
                for tti in range(4):
                    pb, bpb = next_ps()
                    for c2 in range(2):
                        W, bW = Ws[c2]
                        for kt in range(8):
                            mm(pb[:, c2 * 256:(c2 + 1) * 256], HB[:, kt, tti * 128:(tti + 1) * 128], W[:, kt, :], kt == 0, kt == 7,
                               [bW, bHB[kt]], [bpb])
                    act(FC[:, tti, :], pb[:], AF.Silu, [bpb], [bFC[tti]])
                for tti in range(4):
                    for d in range(2):
                        tr(PSB[:, (tti * 2 + d) * 128:(tti * 2 + d + 1) * 128], KT_[:, d, tti * 128:(tti + 1) * 128], IDB[:],
                           [bKT, bID], [bPSB])
                for tti in range(4):
                    act(KZ[:, tti, :], PSB[:, tti * 256:(tti + 1) * 256], AF.Copy, [bPSB, bPAR], [bKZ], scale=pc("zs", h))

            def ret_loop(h, filler):
                par = h % 2
                QT, KT_, VV = QTs[par], KTs[par], VVs[par]
                bQT, bKT, bVV = bQTs[par], bKTs[par], bVVs[par]
                pys = {}

                def g1(tti):
                    c0_ = 64 + tti * 16
                    bsm = bSMR[tti]
                    py, bpy = pys[tti]
                    S.op("dve", (lambda p_, c_: lambda e: e.bn_stats(out=SMALL[:, c_:c_ + 6], in_=p_[:]))(py, c0_), [bpy], [bsm])
                    S.op("dve", (lambda c_: lambda e: e.bn_aggr(out=SMALL[:, c_ + 6:c_ + 8], in_=SMALL[:, c_:c_ + 6]))(c0_), [bsm], [bsm])
                    act(SMALL[:, c0_ + 8:c0_ + 9], SMALL[:, c0_ + 7:c0_ + 8], AF.Sqrt, [bsm, bPAR], [bsm], bias=pc("epsx", h))
                    rn = tti % 2
                    stt(RN[:, rn, :], py[:], SMALL[:, c0_ + 6:c0_ + 7], FC[:, tti, :], ALU.subtract, ALU.mult, [bpy, bsm, bFC[tti]], [bRN[rn]])

                def g2(tti):
                    c0_ = 64 + tti * 16
                    bsm = bSMR[tti]
                    py, bpy = pys[tti]
                    rn = tti % 2
                    S.op("dve", (lambda c_: lambda e: e.reciprocal(out=SMALL[:, c_ + 8:c_ + 9], in_=SMALL[:, c_ + 8:c_ + 9]))(c0_), [bsm], [bsm])
                    act(RB_[:, tti, :], RN[:, rn, :], AF.Copy, [bRN[rn], bsm], [bRB], scale=SMALL[:, c0_ + 8:c0_ + 9])

                def g3(tti):
                    rn = tti % 2
                    pass

                for tti in range(4):
                    tsl = slice(tti * 128, (tti + 1) * 128)
                    psc, bpsc = next_ps()
                    for d in range(2):
                        mm(psc[:, 0:128], KT_[:, d, tsl], QT[:, d, tsl], d == 0, d == 1, [bKT, bQT], [bpsc])
                    tt(SD[:, tti, :], psc[:, 0:128], MK[:, h, :], ALU.mult, [bpsc, bMK], [bSD])
                    py, bpy = py_ps(tti)
                    pys[tti] = (py, bpy)
                    mm(py[:], SD[:, tti, :], VV[:, tti, :], True, False, [bSD, bVV], [bpy])
                    for d in range(2):
                        mm(py[:], QT[:, d, tsl], SBF[:, h, d, :], False, d == 1, [bQT, bSB[h][d]], [bpy])
                    for d in range(2):
                        pk, bpk = next_ps()
                        mm(pk[:], KZ[:, tti, d * 128:(d + 1) * 128], VV[:, tti, :], True, True, [bKZ, bVV], [bpk])
                        stt(SF[:, h, d, :], SF[:, h, d, :], GC[h], pk[:], ALU.mult, ALU.add, [bSF[h][d], bpk], [bSF[h][d]])
                        act(SBF[:, h, d, :], SF[:, h, d, :], AF.Copy, [bSF[h][d]], [bSB[h][d]])
                    g1(tti)
                    if tti >= 1:
                        g2(tti - 1)
                    if tti >= 2:
                        g3(tti - 2)
                    if filler is not None:
                        for _ in range(3 if tti == 0 else 1):
                            next(filler, None)
                g2(3)
                g3(2)
                g3(3)
                for e4 in range(4):
                    for tti in range(4):
                        tr(PSB[:, tti * 128:(tti + 1) * 128], RB_[:, tti, e4 * 128:(e4 + 1) * 128], IDB[:], [bRB, bID], [bPSB])
                    act(RTT[:, h * 4 + e4, :], PSB[:, 0:512], AF.Copy, [bPSB], [bRTT[h * 4 + e4]])

            if first:
                S.op("dve", lambda e: e.memset(UH[:], 0.0), [], [bUH])
                for h in range(H):
                    for d in range(2):
                        S.op("dve", (lambda h_, d_: lambda e: e.memset(SF[:, h_, d_, :], 0.0))(h, d), [], [bSF[h][d]])
                        S.op("dve", (lambda h_, d_: lambda e: e.memset(SBF[:, h_, d_, :], 0.0))(h, d), [], [bSB[h][d]])
            for f in range(8):
                cp(U[:, f, 0:30], UH[:, f, :], [bUH], [bU[f]])
            for half in range(2):
                for cpair in range(2):
                    W, bW = wload("w_in", 4 + half * 2 + cpair)
                    for q in range(2):
                        fi = cpair * 2 + q
                        pb, bpb = next_ps()
                        for kt in range(8):
                            mm(pb[:], W[:, kt, q * 128:(q + 1) * 128], HB[:, kt, :], kt == 0, kt == 7, [bW, bHB[kt]], [bpb])
                        act(FC[:, fi, :], pb[:], AF.Sigmoid, [bpb], [bFC[fi]])
                for cpair in range(2):
                    W, bW = wload("w_in", half * 2 + cpair)
                    for q in range(2):
                        fi = cpair * 2 + q
                        f = half * 4 + fi
                        pb, bpb = next_ps()
                        for kt in range(8):
                            mm(pb[:], W[:, kt, q * 128:(q + 1) * 128], HB[:, kt, :], kt == 0, kt == 7, [bW, bHB[kt]], [bpb])
                        tt(U[:, f, 30:542], pb[:], FC[:, fi, :], ALU.mult, [bpb, bFC[fi]], [bU[f]])
            o_cdw, _ = PCOLS["cdw"]
            for f in range(8):
                g = f % 2
                S.op("dve", (lambda g_, f_: lambda e: e.tensor_tensor(
                    out=DG[:, g_, :, :], in0=IDB[:].unsqueeze(1).broadcast_to([128, 31, 128]),
                    in1=PAR[:, o_cdw + f_ * 31:o_cdw + (f_ + 1) * 31].unsqueeze(2).broadcast_to([128, 31, 128]), op=ALU.mult))(g, f),
                    [bID, bPAR], [bDG[g]])
                pb, bpb = next_ps()
                for kk in range(31):
                    mm(pb[:], DG[:, g, kk, :], U[:, f, kk:kk + 512], kk == 0, kk == 30, [bDG[g], bU[f]], [bpb])
                act(FA[:, f, :], pb[:], AF.Identity, [bpb, bPAR], [bFA[f]], bias=pc("cdb", f))
                act(BA[:, f, :], pb[:], AF.Square, [bpb, bPAR], [bBA[f]], bias=pc("cdb", f))
                act(RTT[:, f, :], pb[:], AF.Identity, [bpb, bPAR], [bRTT[f]], bias=pc("cdb", f))
            for f in range(8):
                cp(UH[:, f, :], U[:, f, 512:542], [bU[f]], [bUH])
            alias_barrier(bU, hid_groups["head"])
            ln_stats(RTT, bRTT, BA, bBA)
            g0 = proj_qkv(0)
            next(g0, None)
            next(g0, None)
            for f in range(8):
                ln_norm(FA, bFA, f)
                act(BA[:, f, :], FA[:, f, :], AF.Silu, [bFA[f], bPAR], [bBA[f]], bias=pc("clb", f), scale=pc("clg", f))
            for _ in g0:
                pass
            for mp in range(4):
                Wg, bWg = wload("w_in", 32 + mp)
                Wc, bWc = wload("w_conv_o", mp)
                for q in range(2):
                    m = mp * 2 + q
                    pg, bpg = next_ps()
                    for kt in range(8):
                        mm(pg[:], Wg[:, kt, q * 128:(q + 1) * 128], HB[:, kt, :], kt == 0, kt == 7, [bWg, bHB[kt]], [bpg])
                    act(FC[:, q, :], pg[:], AF.Sigmoid, [bpg, bPAR], [bFC[q]], bias=pc("bg0", m))
                    pb, bpb = next_ps()
                    for kt in range(8):
                        mm(pb[:], Wc[:, kt, q * 128:(q + 1) * 128], BA[:, kt, :], kt == 0, kt == 7, [bWc, bBA[kt]], [bpb])
                    stt(FA[:, m, :], pb[:], pc("bco", m), FC[:, q, :], ALU.add, ALU.mult, [bpb, bPAR, bFC[q]], [bFA[m]])

            if stop == 'B':
                break
            proj_g_kz(0)
            ps_n[0] = 5
            for h in range(H):
                filler = proj_qkv(h + 1) if h + 1 < H else None
                ret_loop(h, filler)
                if filler is not None:
                    for _ in filler:
                        pass
                    proj_g_kz(h + 1)
            ps_n[0] = 7
            for mp in range(4):
                Wg, bWg = wload("w_in", 36 + mp)
                for q in range(2):
                    m = mp * 2 + q
                    Wr, bWr = wload("w_ret_o", m)
                    pg, bpg = next_ps()
                    for kt in range(8):
                        mm(pg[:], Wg[:, kt, q * 128:(q + 1) * 128], HB[:, kt, :], kt == 0, kt == 7, [bWg, bHB[kt]], [bpg])
                    act(FC[:, q, :], pg[:], AF.Sigmoid, [bpg, bPAR], [bFC[q]], bias=pc("bg1", m))
                    pb, bpb = next_ps()
                    for kt in range(16):
                        mm(pb[:], Wr[:, kt, :], RTT[:, kt, :], kt == 0, kt == 15, [bWr, bRTT[kt]], [bpb])
                    tt(FC[:, 2 + q, :], pb[:], FC[:, q, :], ALU.mult, [bpb, bFC[q]], [bFC[2 + q]])
                    tt(BA[:, m, :], FC[:, 2 + q, :], FA[:, m, :], ALU.add, [bFC[2 + q], bFA[m]], [bBA[m]])

            if stop == 'C':
                break
            for mp in range(4):
                W, bW = wload("w_out", mp)
                for q in range(2):
                    m = mp * 2 + q
                    pb, bpb = next_ps()
                    for kt in range(8):
                        mm(pb[:], W[:, kt, q * 128:(q + 1) * 128], BA[:, kt, :], kt == 0, kt == 7, [bW, bBA[kt]], [bpb])
                    stt(HT32[:, m, :], HT32[:, m, :], float(ALPHA), pb[:], ALU.mult, ALU.add, [bHT[m], bpb], [bHT[m]])
                    act(BB[:, m, :], HT32[:, m, :], AF.Square, [bHT[m]], [bBB[m]])
                    act(RTT[:, m, :], HT32[:, m, :], AF.Copy, [bHT[m]], [bRTT[m]])
            ln_stats(RTT, bRTT, BB, bBB)
            for f in range(8):
                ln_norm(HT32, bHT, f)
                act(BA[:, f, :], HT32[:, f, :], AF.Identity, [bHT[f], bPAR], [bBA[f]], bias=pc("ln1_b", f), scale=pc("ln1_g", f))
                act(HT32[:, f, :], HT32[:, f, :], AF.Identity, [bHT[f], bPAR], [bHT[f]], bias=pc("ln1_b", f), scale=pc("ln1_g", f))
            H1 = BA
            bH1 = bBA

            if stop == 'D':
                break
            alias_barrier(hid_groups["head"], bHID)
            o_fdw, _ = PCOLS["fdw"]
            o_fdb, _ = PCOLS["fdb"]
            if first:
                S.op("dve", lambda e: e.memset(ZHT[:], 0.0), [], bZH)
            ZH = ZHT[:, :].rearrange("p (c t) -> p c t", c=44)
            deferred_halo = []
            deferred_mult = []
            for jp in range(11):
                Wg, bWg = wload("w_ffn_up", jp)
                Wv, bWv = wload("w_ffn_up", 11 + jp)
                for q in range(2):
                    j = jp * 2 + q
                    accs = []
                    for part, (W, bW) in enumerate(((Wg, bWg), (Wv, bWv))):
                        c = j if part == 0 else 22 + j
                        pb, bpb = next_ps()
                        for kt in range(8):
                            mm(pb[:], W[:, kt, q * 128:(q + 1) * 128], H1[:, kt, :], kt == 0, kt == 7, [bW, bH1[kt]], [bpb])
                        aslot = (j % 2) * 3 + part
                        A_ = FA[:, aslot, :]
                        w0 = PAR[:, o_fdw + c * 3 + 0:o_fdw + c * 3 + 1]
                        w1 = PAR[:, o_fdw + c * 3 + 1:o_fdw + c * 3 + 2]
                        w2 = PAR[:, o_fdw + c * 3 + 2:o_fdw + c * 3 + 3]
                        bb_ = PAR[:, o_fdb + c:o_fdb + c + 1]
                        act(A_, pb[:], AF.Identity, [bpb, bPAR], [bFA[aslot]], bias=bb_, scale=w2)
                        act(A_[:, 0:1], ZH[:, c, 1:2], AF.Identity, [bZH[c], bPAR, bFA[aslot]], [bFA[aslot]], bias=A_[:, 0:1], scale=w1)
                        for fn_ in deferred_halo:
                            fn_()
                        deferred_halo = []
                        stt(A_[:, 1:512], pb[:, 0:511], w1, A_[:, 1:512], ALU.mult, ALU.add, [bpb, bPAR, bFA[aslot]], [bFA[aslot]])
                        stt(A_[:, 2:512], pb[:, 0:510], w0, A_[:, 2:512], ALU.mult, ALU.add, [bpb, bPAR, bFA[aslot]], [bFA[aslot]])
                        stt(A_[:, 0:2], ZH[:, c, 0:2], w0, A_[:, 0:2], ALU.mult, ALU.add, [bZH[c], bPAR, bFA[aslot]], [bFA[aslot]])
                        deferred_halo.append((lambda c_, pb_, bpb_: lambda: act(ZH[:, c_, 0:2], pb_[:, 510:512], AF.Copy, [bpb_], [bZH[c_]]))(c, pb, bpb))
                        accs.append((A_, bFA[aslot]))
                    (Ag, bAg), (Av, bAv) = accs
                    sslot = (j % 2) * 3 + 2
                    act(FA[:, sslot, :], Ag, AF.Silu, [bAg], [bFA[sslot]])
                    for fn_ in deferred_mult:
                        fn_()
                    deferred_mult = [(lambda j_, ss_, Av_, bAv_: lambda: tt(HIDV[:, j_, :], FA[:, ss_, :], Av_, ALU.mult, [bFA[ss_], bAv_], [bHID[j_]]))(j, sslot, Av, bAv)]
            for fn_ in deferred_halo + deferred_mult:
                fn_()

            dma("pld", PL[:], p_d[t0:t0 + TB, :].rearrange("(t p) c -> p t c", p=128), [], [bPL])
            for kt in range(2):
                pt_, bpt = next_ps()
                for tti in range(4):
                    tr(pt_[:, tti * 128:(tti + 1) * 128], PL[:, tti, kt * 128:(kt + 1) * 128], IDF[:], [bPL, bID], [bpt])
                act(PT[:, kt, :], pt_[:], AF.Copy, [bpt], [bPT])
            for mp in range(4):
                Wg, bWg = wload("w_ple_gate", mp)
                Wp, bWp = wload("w_ple", mp)
                for q in range(2):
                    m = mp * 2 + q
                    pg, bpg = next_ps()
                    for kt in range(8):
                        mm(pg[:], Wg[:, kt, q * 128:(q + 1) * 128], H1[:, kt, :], kt == 0, kt == 7, [bWg, bH1[kt]], [bpg])
                    act(FC[:, q, :], pg[:], AF.Sigmoid, [bpg, bPAR], [bFC[q]], bias=pc("bpg", m))
                    pp, bpp = next_ps()
                    for kt in range(2):
                        mm(pp[:], Wp[:, kt, q * 128:(q + 1) * 128], PT[:, kt, :], kt == 0, kt == 1, [bWp, bPT], [bpp])
                    tt(FC[:, 2 + q, :], pp[:], FC[:, q, :], ALU.mult, [bpp, bFC[q]], [bFC[2 + q]])
                    pf, bpf = next_ps()
                    for half in range(2):
                        Wd, bWd = wload("w_ffn_down", m * 2 + half)
                        for kt in range(11):
                            j = half * 11 + kt
                            mm(pf[:], Wd[:, kt, :], HIDV[:, j, :], j == 0, j == 21, [bWd, bHID[j]], [bpf])
                    stt(HT32[:, m, :], HT32[:, m, :], float(ALPHA), pf[:], ALU.mult, ALU.add, [bHT[m], bpf], [bHT[m]])
                    tt(HT32[:, m, :], HT32[:, m, :], FC[:, 2 + q, :], ALU.add, [bHT[m], bFC[2 + q]], [bHT[m]])
                    act(BB[:, m, :], HT32[:, m, :], AF.Square, [bHT[m]], [bBB[m]])
                    act(RTT[:, m, :], HT32[:, m, :], AF.Copy, [bHT[m]], [bRTT[m]])
            ln_stats(RTT, bRTT, BB, bBB)
            for f in range(8):
                ln_norm(HT32, bHT, f)
                act(HT32[:, f, :], HT32[:, f, :], AF.Identity, [bHT[f], bPAR], [bHT[f]], bias=pc("ln2_b", f), scale=pc("ln2_g", f))
            for tti in range(4):
                s = tti % 2
                for half in range(2):
                    pt_, bpt = next_ps()
                    for q in range(4):
                        f = half * 4 + q
                        tr(pt_[:, q * 128:(q + 1) * 128], HT32[:, f, tti * 128:(tti + 1) * 128], IDF[:], [bHT[f], bID], [bpt])
                    if half == 0:
                        act(OS[:, s, 0:512], pt_[:], AF.Copy, [bpt], [bOS[s]])
                    else:
                        cp(OS[:, s, 512:1024], pt_[:], [bpt], [bOS[s]])
                r0 = t0 + tti * 128
                out_toks[s] = dma("o%d" % s, out_d[r0:r0 + 128, :], OS[:, s, :], [bOS[s]], [bOUT])
            alias_barrier(bHID, bU)
            alias_barrier(bHID, hid_groups["head"])

        S.wait_all("sp", list(out_toks.values()))
        S.emit()
    return nc


_CACHE = {}


def kernel(**inp):
    inp = {k: np.asarray(v) for k, v in inp.items()}
    P, mk = _pack_params(inp)
    packs = {
        "w_in": _pack_w(np.asarray(inp["w_in"][0], np.float32), 8, 256),
        "w_conv_o": _pack_w(np.asarray(inp["w_conv_o"][0], np.float32), 8, 256),
        "w_out": _pack_w(np.asarray(inp["w_out"][0], np.float32), 8, 256),
        "w_ple_gate": _pack_w(np.asarray(inp["w_ple_gate"][0], np.float32), 8, 256),
        "w_ffn_up": _pack_w(np.asarray(inp["w_ffn_up"][0], np.float32), 8, 256),
        "w_ret_o": _pack_w(np.asarray(inp["w_ret_o"][0], np.float32), 16, 128),
        "w_ffn_down": _pack_wdown(np.asarray(inp["w_ffn_down"][0], np.float32)),
        "w_ple": _pack_w(np.asarray(inp["w_ple"][0], np.float32), 2, 256),
    }
    ident = np.eye(128, dtype=np.float32)
    x = np.asarray(inp["x"], np.float32)
    pos = np.asarray(inp["positions"], np.int32)
    pp = np.asarray(inp["p"][0], np.float32)
    if "nc" not in _CACHE:
        _CACHE["nc"] = build_nc()
    nc = _CACHE["nc"]
    in_maps = []
    for c in range(NCORES):
        m = {
            "x": np.ascontiguousarray(x[2 * c:2 * c + 2].reshape(TOK, D)),
            "pos": np.ascontiguousarray(pos[2 * c:2 * c + 2].reshape(1, TOK)),
            "p": np.ascontiguousarray(pp[2 * c:2 * c + 2].reshape(TOK, 256)),
            "params": P,
            "mask": np.ascontiguousarray(mk.reshape(128, H * 128)),
            "ident": ident,
        }
        m.update(packs)
        in_maps.append(m)
    res = run_bass_kernel_spmd(nc, in_maps, core_ids=list(range(NCORES)))
    out = np.stack([np.asarray(r["out"], np.float32).reshape(2, SEQ, D) for r in res.results], axis=0)
    return out.reshape(16, SEQ, D)
```

```python
import contextlib
import numpy as np
import concourse.bass as bass
import concourse.mybir as mybir
from concourse.bass_utils import run_bass_kernel_spmd

F32 = mybir.dt.float32
BF16 = mybir.dt.bfloat16
I32 = mybir.dt.int32
AF = mybir.ActivationFunctionType
ALU = mybir.AluOpType

NCORES = 8
D = 1024
SEQ = 2048
TB = 512
NBLK = 8
TOK = 4096
H = 4
FFN = 2816
NJ = 22
EPS = 1e-5
ALPHA = 2.0 ** 0.25
COMPUTE = ("pe", "act", "dve", "pool")


class Buf:
    __slots__ = ("name", "last_w", "readers", "excl")

    def __init__(self, name, excl=False):
        self.name = name
        self.last_w = None
        self.readers = []
        self.excl = excl


def alias_barrier(old_bufs, new_bufs):
    toks = []
    for b in old_bufs:
        if b.last_w is not None:
            toks.append(b.last_w)
        toks.extend(b.readers)
    for nb in new_bufs:
        nb.readers.extend(toks)


class Sched:
    def __init__(self, nc):
        self.nc = nc
        self.ops = {e: [] for e in ("pe", "act", "dve", "pool", "sp")}
        self.dma_count = {}
        self.known = {e: {} for e in self.ops}

    def _waits(self, eng, reads, writes, is_dma):
        deps = []
        for b in reads:
            if b.last_w is not None:
                deps.append(b.last_w)
        for b in writes:
            if b.last_w is not None:
                deps.append(b.last_w)
            for r in b.readers:
                deps.append(r)
        out = {}
        for (key, val, kind) in deps:
            if (not is_dma) and kind == "c" and key == eng and eng == "pe":
                continue
            k = (key, kind)
            if k not in out or out[k] < val:
                out[k] = val
        waits = []
        kn = self.known[eng]
        for (key, kind), val in out.items():
            if kn.get((key, kind), -1) >= val:
                continue
            kn[(key, kind)] = val
            waits.append((key, val, kind))
        return waits

    @staticmethod
    def _finish(tok, reads, writes):
        for b in reads:
            b.readers.append(tok)
        for b in writes:
            b.last_w = tok
            b.readers = []

    def op(self, eng, fn, reads=(), writes=()):
        reads = [b for b in reads if b is not None]
        writes = [b for b in writes if b is not None]
        writes = writes + [b for b in reads if b.excl and b not in writes]
        waits = self._waits(eng, reads, writes, False)
        tok = (eng, len(self.ops[eng]), "c")
        self.ops[eng].append({"fn": fn, "waits": waits, "inc": None, "need": False})
        self._finish(tok, reads, writes)
        return tok

    def dma(self, q, sem, fn, reads=(), writes=()):
        reads = [b for b in reads if b is not None]
        writes = [b for b in writes if b is not None]
        waits = self._waits(q, reads, writes, True)
        n = self.dma_count.get(sem, 0) + 1
        self.dma_count[sem] = n
        tok = (sem, 16 * n, "d")
        self.ops[q].append({"fn": fn, "waits": waits, "inc": (sem, 16), "need": True})
        self._finish(tok, reads, writes)
        return tok

    def wait_all(self, eng, toks):
        self.ops[eng].append({"fn": None, "waits": list(toks), "inc": None, "need": False})

    def emit(self):
        nc = self.nc
        for e, lst in self.ops.items():
            for o in lst:
                for (key, val, kind) in o["waits"]:
                    if kind == "c":
                        self.ops[key][val]["need"] = True
        NS = 8
        cum = {}
        for e in COMPUTE:
            c = [0] * NS
            arr = []
            for i, o in enumerate(self.ops[e]):
                if o["need"] and o["inc"] is None:
                    c[i % NS] += 1
                arr.append(c[i % NS])
            cum[e] = arr
        with contextlib.ExitStack() as st:
            sems = {}
            for e in COMPUTE:
                if any(o["need"] and o["inc"] is None for o in self.ops[e]):
                    sems[e] = [st.enter_context(nc.semaphore("s_%s%d" % (e, i))) for i in range(NS)]
            for s in self.dma_count:
                sems[s] = st.enter_context(nc.semaphore("d_" + s))
            block = st.enter_context(nc.Block())

            def run(engname):
                def body(eng):
                    for i, o in enumerate(self.ops[engname]):
                        for (key, val, kind) in o["waits"]:
                            if kind == "c":
                                eng.wait_ge(sems[key][val % NS], cum[key][val])
                            else:
                                eng.wait_ge(sems[key], val)
                        if o["fn"] is None:
                            continue
                        ins = o["fn"](eng)
                        if o["inc"] is not None:
                            ins.then_inc(sems[o["inc"][0]], o["inc"][1])
                        elif o["need"]:
                            ins.then_inc(sems[engname][i % NS], 1)
                return body

            block.tensor(run("pe"))
            block.scalar(run("act"))
            block.vector(run("dve"))
            block.gpsimd(run("pool"))
            block.sync(run("sp"))


WSPEC = {
    "w_in": (8, 256, 40),
    "w_conv_o": (8, 256, 4),
    "w_out": (8, 256, 4),
    "w_ple_gate": (8, 256, 4),
    "w_ffn_up": (8, 256, 22),
    "w_ret_o": (16, 128, 8),
    "w_ffn_down": (11, 128, 16),
    "w_ple": (2, 256, 4),
}


def _pack_w(w, kt, c):
    K, N = w.shape
    nch = N // c
    a = w.reshape(kt, 128, nch, c)
    return np.ascontiguousarray(a.transpose(2, 1, 0, 3)).reshape(nch, 128, kt * c)


def _pack_wdown(w):
    a = w.reshape(2, 11, 128, 8, 128)
    return np.ascontiguousarray(a.transpose(3, 0, 2, 1, 4)).reshape(16, 128, 11 * 128)


PCOLS = {}


def _pcol(name, n, _state=[0]):
    PCOLS[name] = (_state[0], n)
    _state[0] += n


for _n, _w in [("ln0_g", 8), ("ln0_b", 8), ("bg0", 8), ("bg1", 8), ("cdw", 8 * 31), ("cdb", 8), ("clg", 8), ("clb", 8),
               ("bco", 8), ("gng", 16), ("ln1_g", 8), ("ln1_b", 8), ("fdw", 44 * 3), ("fdb", 44), ("bpg", 8),
               ("ln2_g", 8), ("ln2_b", 8), ("zs", 4), ("epsx", 4), ("invf", 1), ("eps", 1), ("halfpi", 1)]:
    _pcol(_n, _w)
NP = sum(v[1] for v in PCOLS.values())


def _vec(v):
    return np.ascontiguousarray(np.asarray(v, np.float32).reshape(-1, 128).T)


def _consts():
    gam = 1.0 - 2.0 ** (-5.0 - np.arange(H, dtype=np.float64))
    lg = np.log(gam)
    idx = np.arange(128, dtype=np.float64)
    i = idx[None, :]
    j = idx[:, None]
    ci = (i // 64)
    cj = (j // 64)
    mk = np.zeros((128, H, 128), np.float64)
    for h in range(H):
        w = np.where(ci == cj, np.exp(lg[h] * np.abs(i - j)), np.where(ci > cj, np.exp(lg[h] * (i - j)), 0.0))
        w = w * np.exp(-lg[h] * (i + 1.0)) / 16.0
        mk[:, h, :] = w
    zs = np.exp(lg[None, :] * (127.0 - idx[:, None])) / 16.0
    epsx = EPS * np.exp(-2.0 * lg[None, :] * (idx[:, None] + 1.0))
    gc = np.exp(lg * 128.0)
    invf = 10000.0 ** (-np.arange(128, dtype=np.float32) / np.float32(128))
    return mk.astype(np.float32), zs.astype(np.float32), epsx.astype(np.float32), [float(g) for g in gc], invf.astype(np.float32)


def _pack_params(inp):
    P = np.zeros((128, NP), np.float32)

    def put(name, arr):
        o, n = PCOLS[name]
        assert arr.shape == (128, n), (name, arr.shape, n)
        P[:, o:o + n] = arr

    put("ln0_g", _vec(inp["ln0_g"]))
    put("ln0_b", _vec(inp["ln0_b"]))
    put("bg0", _vec(inp["b_gate"][0, 0]))
    put("bg1", _vec(inp["b_gate"][0, 1]))
    cw = np.asarray(inp["conv_dw_w"][0], np.float32)
    put("cdw", np.ascontiguousarray(cw.reshape(31, 8, 128).transpose(2, 1, 0)).reshape(128, 8 * 31))
    put("cdb", _vec(inp["conv_dw_b"][0]))
    put("clg", _vec(inp["conv_ln_g"][0]))
    put("clb", _vec(inp["conv_ln_b"][0]))
    put("bco", _vec(inp["b_conv_o"][0]))
    put("gng", _vec(np.asarray(inp["ret_gn_g"][0]).reshape(-1)))
    put("ln1_g", _vec(inp["ln1_g"][0]))
    put("ln1_b", _vec(inp["ln1_b"][0]))
    fw = np.asarray(inp["ffn_dw_w"][0], np.float32)
    put("fdw", np.ascontiguousarray(fw.reshape(3, 44, 128).transpose(2, 1, 0)).reshape(128, 44 * 3))
    put("fdb", _vec(inp["ffn_dw_b"][0]))
    put("bpg", _vec(inp["b_ple_gate"][0]))
    put("ln2_g", _vec(inp["ln2_g"][0]))
    put("ln2_b", _vec(inp["ln2_b"][0]))
    mk, zs, epsx, gc, invf = _consts()
    put("zs", zs)
    put("epsx", epsx)
    put("invf", invf.reshape(128, 1))
    put("eps", np.full((128, 1), EPS, np.float32))
    put("halfpi", np.full((128, 1), np.pi / 2, np.float32))
    return P, mk


def build_nc(nblk=NBLK, stop=None):
    nc = bass.Bass("TRN2", target_bir_lowering=False)
    _, _, _, GC, _ = _consts()

    x_d = nc.dram_tensor("x", [TOK, D], F32, kind="ExternalInput").ap()
    pos_d = nc.dram_tensor("pos", [1, TOK], I32, kind="ExternalInput").ap()
    p_d = nc.dram_tensor("p", [TOK, 256], F32, kind="ExternalInput").ap()
    par_d = nc.dram_tensor("params", [128, NP], F32, kind="ExternalInput").ap()
    mk_d = nc.dram_tensor("mask", [128, H * 128], F32, kind="ExternalInput").ap()
    id_d = nc.dram_tensor("ident", [128, 128], F32, kind="ExternalInput").ap()
    wd = {}
    ws = {}
    for name, (kt, c, nch) in WSPEC.items():
        wd[name] = nc.dram_tensor(name, [nch, 128, kt * c], F32, kind="ExternalInput").ap()
        ws[name] = nc.dram_tensor(name + "_bf", [nch, 128, kt * c], BF16).ap()
    out_d = nc.dram_tensor("out", [TOK, D], F32, kind="ExternalOutput").ap()

    S = Sched(nc)
    with contextlib.ExitStack() as st:
        def sb(name, shape, dt):
            return st.enter_context(nc.sbuf_tensor(name, shape, dt))

        def ps(name, shape, dt):
            return st.enter_context(nc.psum_tensor(name, shape, dt))

        PAR = sb("PAR", [128, NP], F32)
        MK = sb("MK", [128, H, 128], F32)
        IDF = sb("IDF", [128, 128], F32)
        IDB = sb("IDB", [128, 128], BF16)
        ONF = sb("ONF", [128, 128], F32)
        ONB = sb("ONB", [128, 128], BF16)
        HT32 = sb("HT32", [128, 8, TB], F32)
        FA = sb("FA", [128, 8, TB], F32)
        FC = sb("FC", [128, 4, TB], F32)
        XS = sb("XS", [128, 2, D], F32)
        ST = sb("ST", [128, 3, TB], F32)
        CS = sb("CS", [128, 2, TB], F32)
        RT = sb("RT", [128, 2, TB], F32)
        RN = sb("RN", [128, 2, TB], F32)
        SF = sb("SF", [128, H, 2, 512], F32)
        PL = sb("PL", [128, 4, 256], F32)
        OS = sb("OS", [128, 2, D], F32)
        SMALL = sb("SMALL", [128, 128], F32)
        POSI = sb("POSI", [128, TB], I32)
        BA = sb("BA", [128, 8, TB], BF16)
        BB = sb("BB", [128, 8, TB], BF16)
        HID = sb("HID", [128, NJ * TB], BF16)
        UH = sb("UH", [128, 8, 30], BF16)
        DG = sb("DG", [128, 2, 31, 128], BF16)
        RTT = sb("RTT", [128, 16, TB], BF16)
        SBF = sb("SBF", [128, H, 2, 512], BF16)
        WR = sb("WR", [128, 4, 2048], BF16)
        PT = sb("PT", [128, 2, TB], BF16)
        SDT = sb("SDT", [128, 4, 128], BF16)
        ZHT = sb("ZHT", [128, 176], F32)
        PSF = [ps("PS%d" % i, [128, 512], F32) for i in range(7)]
        PSB = ps("PSB", [128, 1024], BF16)
        bPS = [Buf("ps%d" % i, True) for i in range(7)]
        bPSB = Buf("psb", True)
        ps_rr = [0]

        ps_n = [7]

        def next_ps():
            i = ps_rr[0] % ps_n[0]
            ps_rr[0] += 1
            return PSF[i], bPS[i]

        def py_ps(t):
            return PSF[5 + t % 2], bPS[5 + t % 2]

        def pc(name, f=None, n=1):
            o, w = PCOLS[name]
            if f is None:
                return PAR[:, o:o + w]
            return PAR[:, o + f:o + f + n]

        def mm(out, lhsT, rhs, start, stop, r, w):
            S.op("pe", lambda e: e.matmul(out, lhsT=lhsT, rhs=rhs, start=start, stop=stop), r, w)

        def tr(out, in_, ident, r, w):
            S.op("pe", lambda e: e.transpose(out, in_, ident), r, w)

        def act(out, in_, func, r, w, bias=None, scale=None):
            kw = {}
            if bias is not None:
                kw["bias"] = bias
            if scale is not None:
                kw["scale"] = scale
            S.op("act", lambda e: e.activation(out=out, in_=in_, func=func, **kw), r, w)

        def tt(out, in0, in1, op, r, w):
            S.op("dve", lambda e: e.tensor_tensor(out=out, in0=in0, in1=in1, op=op), r, w)

        def ts(out, in0, s1, s2, op0, op1, r, w):
            if s2 is None:
                S.op("dve", lambda e: e.tensor_scalar(out=out, in0=in0, scalar1=s1, scalar2=None, op0=op0), r, w)
            else:
                S.op("dve", lambda e: e.tensor_scalar(out=out, in0=in0, scalar1=s1, scalar2=s2, op0=op0, op1=op1), r, w)

        def stt(out, in0, scalar, in1, op0, op1, r, w):
            S.op("dve", lambda e: e.scalar_tensor_tensor(out=out, in0=in0, scalar=scalar, in1=in1, op0=op0, op1=op1), r, w)

        def cp(out, in_, r, w):
            S.op("dve", lambda e: e.tensor_copy(out=out, in_=in_), r, w)

        def dma(sem, out, in_, r, w):
            return S.dma("sp", sem, lambda e: e.dma_start(out=out, in_=in_), r, w)

        bPAR, bMK, bID, bON = Buf("par"), Buf("mk"), Buf("id"), Buf("on")
        bHT = [Buf("ht%d" % f) for f in range(8)]
        bFA = [Buf("fa%d" % f) for f in range(8)]
        bFC = [Buf("fc%d" % f) for f in range(4)]
        bXS = [Buf("xs%d" % i) for i in range(2)]
        bST = [Buf("st%d" % i) for i in range(3)]
        bCS, bRT, bRN = Buf("cs"), [Buf("rt0"), Buf("rt1")], [Buf("rn0"), Buf("rn1")]
        bSMA = [Buf("sma%d" % i) for i in range(4)]
        bSMR = [Buf("smr%d" % i) for i in range(4)]
        bSF = [[Buf("sf%d%d" % (h, d)) for d in range(2)] for h in range(H)]
        bSB = [[Buf("sb%d%d" % (h, d)) for d in range(2)] for h in range(H)]
        bPL, bOS, bSM, bPOSI = Buf("pl"), [Buf("os0"), Buf("os1")], Buf("small"), Buf("posi")
        bBA = [Buf("ba%d" % f) for f in range(8)]
        bBB = [Buf("bb%d" % f) for f in range(8)]
        bUH, bDG = Buf("uh"), [Buf("dg0"), Buf("dg1")]
        bRTT = [Buf("rtt%d" % i) for i in range(16)]
        bWR = [Buf("wr%d" % i) for i in range(4)]
        bPT = Buf("pt")
        bWS = {name: [Buf("ws_%s%d" % (name, i)) for i in range(WSPEC[name][2])] for name in WSPEC}
        bZH = [Buf("zh%d" % i) for i in range(44)]
        out_toks = {}
        bOUT = Buf("out")
        U = HID[:, 0:8 * 542].rearrange("p (f t) -> p f t", f=8)
        bU = [Buf("u%d" % f) for f in range(8)]
        QTs = [HID[:, 1024 * i:1024 * (i + 1)].rearrange("p (d t) -> p d t", d=2) for i in range(2)]
        KTs = [HID[:, 2048 + 1024 * i:2048 + 1024 * (i + 1)].rearrange("p (d t) -> p d t", d=2) for i in range(2)]
        KZ = HID[:, 4096:5120].rearrange("p (t d) -> p t d", t=4)
        VVs = [HID[:, 5120 + 2048 * i:5120 + 2048 * (i + 1)].rearrange("p (t e) -> p t e", t=4) for i in range(2)]
        RB_ = HID[:, 9216:11264].rearrange("p (t e) -> p t e", t=4)
        SD = SDT
        bQTs, bKTs, bVVs = [Buf("qt0"), Buf("qt1")], [Buf("kt0"), Buf("kt1")], [Buf("vv0"), Buf("vv1")]
        bKZ, bSD, bRB = Buf("kz"), Buf("sd"), Buf("rb")
        HIDV = HID[:, :].rearrange("p (j t) -> p j t", j=NJ)
        bHID = [Buf("hid%d" % j) for j in range(NJ)]
        hid_groups = {"u": bU, "head": bQTs + bKTs + bVVs + [bKZ, bRB], "hid": bHID}

        dma("c0", PAR[:], par_d, [], [bPAR])
        dma("c1", MK[:].rearrange("p h i -> p (h i)"), mk_d, [], [bMK])
        dma("c2", IDF[:], id_d, [], [bID])
        cp(IDB[:], IDF[:], [bID], [bID])
        S.op("dve", lambda e: e.memset(ONF[:], 1.0 / 1024), [], [bON])
        S.op("dve", lambda e: e.memset(ONB[:], 1.0 / 1024), [], [bON])

        stg_f = [FA[:, 0:4, :].rearrange("p a t -> p (a t)"), FA[:, 4:8, :].rearrange("p a t -> p (a t)"),
                 HT32[:, 0:4, :].rearrange("p a t -> p (a t)"), HT32[:, 4:8, :].rearrange("p a t -> p (a t)")]
        bstg_f = [Buf("stgf%d" % i) for i in range(4)]
        stg_b = [HID[:, 2048 * i:2048 * (i + 1)] for i in range(4)]
        bstg_b = [Buf("stgb%d" % i) for i in range(4)]
        order = ["w_in", "w_conv_o", "w_ret_o", "w_out", "w_ffn_up", "w_ple_gate", "w_ple", "w_ffn_down"]
        chunks = [(name, ci) for name in order for ci in range(WSPEC[name][2])]

        def pro_load(k):
            name, ci = chunks[k]
            kt, c, nch = WSPEC[name]
            s = k % 4
            dma("pl%d" % s, stg_f[s][:, 0:kt * c], wd[name][ci], [], [bstg_f[s]])

        LOOK = 3
        for k in range(min(LOOK, len(chunks))):
            pro_load(k)
        for k, (name, ci) in enumerate(chunks):
            kt, c, nch = WSPEC[name]
            n = kt * c
            s = k % 4
            if name == "w_ret_o":
                o_, _ = PCOLS["gng"]
                S.op("dve", (lambda sf, sbf: lambda e: e.tensor_tensor(
                    out=sbf.rearrange("p (k c) -> p k c", k=16), in0=sf.rearrange("p (k c) -> p k c", k=16),
                    in1=PAR[:, o_:o_ + 16].unsqueeze(2).broadcast_to([128, 16, 128]), op=ALU.mult))(stg_f[s][:, 0:n], stg_b[s][:, 0:n]),
                    [bstg_f[s], bPAR], [bstg_b[s]])
            elif k % 2 == 0:
                act(stg_b[s][:, 0:n], stg_f[s][:, 0:n], AF.Copy, [bstg_f[s]], [bstg_b[s]])
            else:
                cp(stg_b[s][:, 0:n], stg_f[s][:, 0:n], [bstg_f[s]], [bstg_b[s]])
            dma("ps%d" % s, ws[name][ci], stg_b[s][:, 0:n], [bstg_b[s]], [bWS[name][ci]])
            if k + LOOK < len(chunks):
                pro_load(k + LOOK)
        alias_barrier(bstg_f, bFA)
        alias_barrier(bstg_f, bHT)
        alias_barrier(bstg_b, bU + hid_groups["head"] + bHID)

        wr_rr = [0]

        def wload(name, ci):
            kt, c, nch = WSPEC[name]
            n = kt * c
            s = wr_rr[0] % 4
            wr_rr[0] += 1
            dma("w%d" % s, WR[:, s, 0:n], ws[name][ci], [bWS[name][ci]], [bWR[s]])
            return WR[:, s, 0:n].rearrange("p (k c) -> p k c", k=kt), bWR[s]

        def ln_stats(SRCB, bSRCB, SQ, bSQ):
            pm, bpm = next_ps()
            for f in range(8):
                mm(pm[:], ONB[:], SRCB[:, f, :], f == 0, f == 7, [bON, bSRCB[f]], [bpm])
            pe_, bpe = next_ps()
            for f in range(8):
                mm(pe_[:], ONB[:], SQ[:, f, :], f == 0, f == 7, [bON, bSQ[f]], [bpe])
            act(ST[:, 0, :], pm[:], AF.Copy, [bpm], [bST[0]])
            act(ST[:, 1, :], pm[:], AF.Square, [bpm], [bST[1]])
            tt(ST[:, 1, :], pe_[:], ST[:, 1, :], ALU.subtract, [bpe, bST[1]], [bST[1]])
            act(ST[:, 1, :], ST[:, 1, :], AF.Sqrt, [bST[1], bPAR], [bST[1]], bias=pc("eps", 0))
            S.op("dve", lambda e: e.reciprocal(out=ST[:, 1, :], in_=ST[:, 1, :]), [bST[1]], [bST[1]])

        def ln_norm(SRC, bSRC, f):
            tt(SRC[:, f, :], SRC[:, f, :], ST[:, 0, :], ALU.subtract, [bSRC[f], bST[0]], [bSRC[f]])
            tt(SRC[:, f, :], SRC[:, f, :], ST[:, 1, :], ALU.mult, [bSRC[f], bST[1]], [bSRC[f]])

        C1 = 6.28125
        C2 = 2 * np.pi - 6.28125

        for s_ in range(2):
            dma("x%d" % s_, XS[:, s_, :], x_d[s_ * 128:(s_ + 1) * 128, :], [], [bXS[s_]])
        for blk in range(nblk):
            t0 = blk * TB
            first = (blk % 4 == 0)
            if stop == 'P':
                break
            if stop == 'pos':
                break
            deferred_cp = []
            for tti in range(4):
                s = tti % 2
                r0 = t0 + tti * 128
                if tti >= 2:
                    dma("x%d" % s, XS[:, s, :], x_d[r0:r0 + 128, :], [], [bXS[s]])
                c0 = tti * 16
                bsm = bSMA[tti]
                S.op("dve", (lambda s_, c_: lambda e: e.bn_stats(out=SMALL[:, c_:c_ + 6], in_=XS[:, s_, 0:512]))(s, c0), [bXS[s]], [bsm])
                S.op("dve", (lambda s_, c_: lambda e: e.bn_stats(out=SMALL[:, c_ + 6:c_ + 12], in_=XS[:, s_, 512:1024]))(s, c0), [bXS[s]], [bsm])
                S.op("dve", (lambda c_: lambda e: e.bn_aggr(out=SMALL[:, c_ + 12:c_ + 14], in_=SMALL[:, c_:c_ + 12]))(c0), [bsm], [bsm])
                act(SMALL[:, c0 + 14:c0 + 15], SMALL[:, c0 + 13:c0 + 14], AF.Sqrt, [bsm, bPAR], [bsm], bias=pc("eps", 0))
                S.op("dve", (lambda c_: lambda e: e.reciprocal(out=SMALL[:, c_ + 14:c_ + 15], in_=SMALL[:, c_ + 14:c_ + 15]))(c0), [bsm], [bsm])
                ts(XS[:, s, :], XS[:, s, :], SMALL[:, c0 + 12:c0 + 13], SMALL[:, c0 + 14:c0 + 15], ALU.subtract, ALU.mult, [bXS[s], bsm], [bXS[s]])
                for fn_ in deferred_cp:
                    fn_()
                deferred_cp = []
                if stop == 'A1':
                    continue
                for half in range(2):
                    pt_, bpt = next_ps()
                    for q in range(4):
                        f = half * 4 + q
                        tr(pt_[:, q * 128:(q + 1) * 128], XS[:, s, f * 128:(f + 1) * 128], IDF[:], [bXS[s], bID], [bpt])
                    if stop == 'A2':
                        continue
                    for q in range(4):
                        f = half * 4 + q
                        act(HT32[:, f, tti * 128:(tti + 1) * 128], pt_[:, q * 128:(q + 1) * 128], AF.Identity,
                            [bpt, bPAR], [bHT[f]], bias=pc("ln0_b", f), scale=pc("ln0_g", f))
                        deferred_cp.append((lambda f_, t_: lambda: cp(BB[:, f_, t_ * 128:(t_ + 1) * 128], HT32[:, f_, t_ * 128:(t_ + 1) * 128], [bHT[f_]], [bBB[f_]]))(f, tti))
            for fn_ in deferred_cp:
                fn_()
            HB = BB
            bHB = bBB
            if blk + 1 < nblk:
                for s_ in range(2):
                    rn_ = (blk + 1) * TB + s_ * 128
                    dma("x%d" % s_, XS[:, s_, :], x_d[rn_:rn_ + 128, :], [], [bXS[s_]])

            if stop in ('A', 'A1', 'A2'):
                break
            dma("pos", POSI[:], pos_d[:, t0:t0 + TB].partition_broadcast(128), [], [bPOSI])
            cp(RT[:, 0, :], POSI[:], [bPOSI], [bRT[0]])
            ts(RT[:, 0, :], RT[:, 0, :], pc("invf", 0), None, ALU.mult, None, [bRT[0], bPAR], [bRT[0]])
            for which in range(2):
                if which == 0:
                    ts(RT[:, 1, :], RT[:, 0, :], pc("halfpi", 0), None, ALU.add, None, [bRT[0], bPAR], [bRT[1]])
                    src = RT[:, 1, :]
                    bsrc = bRT[1]
                else:
                    src = RT[:, 0, :]
                    bsrc = bRT[0]
                ts(POSI[:], src, float(1 / (2 * np.pi)), None, ALU.mult, None, [bsrc], [bPOSI])
                cp(CS[:, which, :], POSI[:], [bPOSI], [bCS])
                stt(src, CS[:, which, :], -C1, src, ALU.mult, ALU.add, [bCS, bsrc], [bsrc])
                stt(src, CS[:, which, :], -C2, src, ALU.mult, ALU.add, [bCS, bsrc], [bsrc])
                S.op("dve", (lambda o, i: lambda e: e.tensor_single_scalar(out=o, in_=i, scalar=float(np.pi), op=ALU.is_gt))(CS[:, which, :], src),
                     [bsrc], [bCS])
                stt(src, CS[:, which, :], float(-2 * np.pi), src, ALU.mult, ALU.add, [bCS, bsrc], [bsrc])
                S.op("dve", (lambda o, i: lambda e: e.tensor_single_scalar(out=o, in_=i, scalar=float(-np.pi), op=ALU.is_lt))(CS[:, which, :], src),
                     [bsrc], [bCS])
                stt(src, CS[:, which, :], float(2 * np.pi), src, ALU.mult, ALU.add, [bCS, bsrc], [bsrc])
                act(CS[:, which, :], src, AF.Sin, [bsrc], [bCS])
            COS = CS[:, 0, :]
            SIN = CS[:, 1, :]

            def proj_qkv(h):
                par = h % 2
                QT, KT_, VV = QTs[par], KTs[par], VVs[par]
                bQT, bKT, bVV = bQTs[par], bKTs[par], bVVs[par]
                Ws = [wload("w_in", 16 + 2 * h + c2) for c2 in range(2)]
                for tti in range(4):
                    pb, bpb = next_ps()
                    for c2 in range(2):
                        W, bW = Ws[c2]
                        for kt in range(8):
                            mm(pb[:, c2 * 256:(c2 + 1) * 256], HB[:, kt, tti * 128:(tti + 1) * 128], W[:, kt, :], kt == 0, kt == 7,
                               [bW, bHB[kt]], [bpb])
                    act(VV[:, tti, :], pb[:], AF.Copy, [bpb], [bVV])
                    if tti % 2 == 1:
                        yield
                for which, (dst, bdst, cbase) in enumerate(((QT, bQT, 8 + h), (KT_, bKT, 12 + h))):
                    W, bW = wload("w_in", cbase)
                    pa, bpa = next_ps()
                    pbb, bpbb = next_ps()
                    for kt in range(8):
                        mm(pa[:], W[:, kt, 0:128], HB[:, kt, :], kt == 0, kt == 7, [bW, bHB[kt]], [bpa])
                    for kt in range(8):
                        mm(pbb[:], W[:, kt, 128:256], HB[:, kt, :], kt == 0, kt == 7, [bW, bHB[kt]], [bpbb])
                    tt(RT[:, 0, :], pa[:], COS, ALU.mult, [bpa, bCS], [bRT[0]])
                    tt(RT[:, 1, :], pbb[:], SIN, ALU.mult, [bpbb, bCS], [bRT[1]])
                    tt(dst[:, 0, :], RT[:, 0, :], RT[:, 1, :], ALU.subtract, [bRT[0], bRT[1]], [bdst])
                    yield
                    tt(RT[:, 0, :], pbb[:], COS, ALU.mult, [bpbb, bCS], [bRT[0]])
                    tt(RT[:, 1, :], pa[:], SIN, ALU.mult, [bpa, bCS], [bRT[1]])
                    tt(dst[:, 1, :], RT[:, 0, :], RT[:, 1, :], ALU.add, [bRT[0], bRT[1]], [bdst])
                    yield

            def proj_g_kz(h):
                par = h % 2
                KT_, bKT = KTs[par], bKTs[par]
                Ws = [wload("w_in", 24 + 2 * h + c2) for c2 in range(2)]
                for tti in range(4):
                    pb, bpb = next_ps()
                    for c2 in range(2):
                        W, bW = Ws[c2]
                        for kt in range(8):
                            mm(pb[:, c2 * 256:(c2 + 1) * 256], HB[:, kt, tti * 128:(tti + 1) * 128], W[:, kt, :], kt == 0, kt == 7,
                               [bW, bHB[kt]], [bpb])
                    act(FC[:, tti, :], pb[:], AF.Silu, [bpb], [bFC[tti]])
                for tti in range(4):
                    for d in range(2):
                        tr(PSB[:, (tti * 2 + d) * 128:(tti * 2 + d + 1) * 128], KT_[:, d, tti * 128:(tti + 1) * 128], IDB[:],
                           [bKT, bID], [bPSB])
                for tti in range(4):
                    act(KZ[:, tti, :], PSB[:, tti * 256:(tti + 1) * 256], AF.Copy, [bPSB, bPAR], [bKZ], scale=pc("zs", h))

            def ret_loop(h, filler):
                par = h % 2
                QT, KT_, VV = QTs[par], KTs[par], VVs[par]
                bQT, bKT, bVV = bQTs[par], bKTs[par], bVVs[par]
                pys = {}

                def g1(tti):
                    c0_ = 64 + tti * 16
                    bsm = bSMR[tti]
                    py, bpy = pys[tti]
                    S.op("dve", (lambda p_, c_: lambda e: e.bn_stats(out=SMALL[:, c_:c_ + 6], in_=p_[:]))(py, c0_), [bpy], [bsm])
                    S.op("dve", (lambda c_: lambda e: e.bn_aggr(out=SMALL[:, c_ + 6:c_ + 8], in_=SMALL[:, c_:c_ + 6]))(c0_), [bsm], [bsm])
                    act(SMALL[:, c0_ + 8:c0_ + 9], SMALL[:, c0_ + 7:c0_ + 8], AF.Sqrt, [bsm, bPAR], [bsm], bias=pc("epsx", h))
                    rn = tti % 2
                    stt(RN[:, rn, :], py[:], SMALL[:, c0_ + 6:c0_ + 7], FC[:, tti, :], ALU.subtract, ALU.mult, [bpy, bsm, bFC[tti]], [bRN[rn]])

                def g2(tti):
                    c0_ = 64 + tti * 16
                    bsm = bSMR[tti]
                    py, bpy = pys[tti]
                    rn = tti % 2
                    S.op("dve", (lambda c_: lambda e: e.reciprocal(out=SMALL[:, c_ + 8:c_ + 9], in_=SMALL[:, c_ + 8:c_ + 9]))(c0_), [bsm], [bsm])
                    act(RB_[:, tti, :], RN[:, rn, :], AF.Copy, [bRN[rn], bsm], [bRB], scale=SMALL[:, c0_ + 8:c0_ + 9])

                def g3(tti):
                    rn = tti % 2
                    pass

                for tti in range(4):
                    tsl = slice(tti * 128, (tti + 1) * 128)
                    psc, bpsc = next_ps()
                    for d in range(2):
                        mm(psc[:, 0:128], KT_[:, d, tsl], QT[:, d, tsl], d == 0, d == 1, [bKT, bQT], [bpsc])
                    tt(SD[:, tti, :], psc[:, 0:128], MK[:, h, :], ALU.mult, [bpsc, bMK], [bSD])
                    py, bpy = py_ps(tti)
                    pys[tti] = (py, bpy)
                    mm(py[:], SD[:, tti, :], VV[:, tti, :], True, False, [bSD, bVV], [bpy])
                    for d in range(2):
                        mm(py[:], QT[:, d, tsl], SBF[:, h, d, :], False, d == 1, [bQT, bSB[h][d]], [bpy])
                    for d in range(2):
                        pk, bpk = next_ps()
                        mm(pk[:], KZ[:, tti, d * 128:(d + 1) * 128], VV[:, tti, :], True, True, [bKZ, bVV], [bpk])
                        stt(SF[:, h, d, :], SF[:, h, d, :], GC[h], pk[:], ALU.mult, ALU.add, [bSF[h][d], bpk], [bSF[h][d]])
                        act(SBF[:, h, d, :], SF[:, h, d, :], AF.Copy, [bSF[h][d]], [bSB[h][d]])
                    g1(tti)
                    if tti >= 1:
                        g2(tti - 1)
                    if tti >= 2:
                        g3(tti - 2)
                    if filler is not None:
                        for _ in range(3 if tti == 0 else 1):
                            next(filler, None)
                g2(3)
                g3(2)
                g3(3)
                for e4 in range(4):
                    for tti in range(4):
                        tr(PSB[:, tti * 128:(tti + 1) * 128], RB_[:, tti, e4 * 128:(e4 + 1) * 128], IDB[:], [bRB, bID], [bPSB])
                    act(RTT[:, h * 4 + e4, :], PSB[:, 0:512], AF.Copy, [bPSB], [bRTT[h * 4 + e4]])

            if first:
                S.op("dve", lambda e: e.memset(UH[:], 0.0), [], [bUH])
                for h in range(H):
                    for d in range(2):
                        S.op("dve", (lambda h_, d_: lambda e: e.memset(SF[:, h_, d_, :], 0.0))(h, d), [], [bSF[h][d]])
                        S.op("dve", (lambda h_, d_: lambda e: e.memset(SBF[:, h_, d_, :], 0.0))(h, d), [], [bSB[h][d]])
            for f in range(8):
                cp(U[:, f, 0:30], UH[:, f, :], [bUH], [bU[f]])
            for half in range(2):
                for cpair in range(2):
                    W, bW = wload("w_in", 4 + half * 2 + cpair)
                    for q in range(2):
                        fi = cpair * 2 + q
                        pb, bpb = next_ps()
                        for kt in range(8):
                            mm(pb[:], W[:, kt, q * 128:(q + 1) * 128], HB[:, kt, :], kt == 0, kt == 7, [bW, bHB[kt]], [bpb])
                        act(FC[:, fi, :], pb[:], AF.Sigmoid, [bpb], [bFC[fi]])
                for cpair in range(2):
                    W, bW = wload("w_in", half * 2 + cpair)
                    for q in range(2):
                        fi = cpair * 2 + q
                        f = half * 4 + fi
                        pb, bpb = next_ps()
                        for kt in range(8):
                            mm(pb[:], W[:, kt, q * 128:(q + 1) * 128], HB[:, kt, :], kt == 0, kt == 7, [bW, bHB[kt]], [bpb])
                        tt(U[:, f, 30:542], pb[:], FC[:, fi, :], ALU.mult, [bpb, bFC[fi]], [bU[f]])
            o_cdw, _ = PCOLS["cdw"]
            for f in range(8):
                g = f % 2
                S.op("dve", (lambda g_, f_: lambda e: e.tensor_tensor(
                    out=DG[:, g_, :, :], in0=IDB[:].unsqueeze(1).broadcast_to([128, 31, 128]),
                    in1=PAR[:, o_cdw + f_ * 31:o_cdw + (f_ + 1) * 31].unsqueeze(2).broadcast_to([128, 31, 128]), op=ALU.mult))(g, f),
                    [bID, bPAR], [bDG[g]])
                pb, bpb = next_ps()
                for kk in range(31):
                    mm(pb[:], DG[:, g, kk, :], U[:, f, kk:kk + 512], kk == 0, kk == 30, [bDG[g], bU[f]], [bpb])
                act(FA[:, f, :], pb[:], AF.Identity, [bpb, bPAR], [bFA[f]], bias=pc("cdb", f))
                act(BA[:, f, :], pb[:], AF.Square, [bpb, bPAR], [bBA[f]], bias=pc("cdb", f))
                act(RTT[:, f, :], pb[:], AF.Identity, [bpb, bPAR], [bRTT[f]], bias=pc("cdb", f))
            for f in range(8):
                cp(UH[:, f, :], U[:, f, 512:542], [bU[f]], [bUH])
            alias_barrier(bU, hid_groups["head"])
            ln_stats(RTT, bRTT, BA, bBA)
            g0 = proj_qkv(0)
            next(g0, None)
            next(g0, None)
            for f in range(8):
                ln_norm(FA, bFA, f)
                act(BA[:, f, :], FA[:, f, :], AF.Silu, [bFA[f], bPAR], [bBA[f]], bias=pc("clb", f), scale=pc("clg", f))
            for _ in g0:
                pass
            for mp in range(4):
                Wg, bWg = wload("w_in", 32 + mp)
                Wc, bWc = wload("w_conv_o", mp)
                for q in range(2):
                    m = mp * 2 + q
                    pg, bpg = next_ps()
                    for kt in range(8):
                        mm(pg[:], Wg[:, kt, q * 128:(q + 1) * 128], HB[:, kt, :], kt == 0, kt == 7, [bWg, bHB[kt]], [bpg])
                    act(FC[:, q, :], pg[:], AF.Sigmoid, [bpg, bPAR], [bFC[q]], bias=pc("bg0", m))
                    pb, bpb = next_ps()
                    for kt in range(8):
                        mm(pb[:], Wc[:, kt, q * 128:(q + 1) * 128], BA[:, kt, :], kt == 0, kt == 7, [bWc, bBA[kt]], [bpb])
                    stt(FA[:, m, :], pb[:], pc("bco", m), FC[:, q, :], ALU.add, ALU.mult, [bpb, bPAR, bFC[q]], [bFA[m]])

            if stop == 'B':
                break
            proj_g_kz(0)
            ps_n[0] = 5
            for h in range(H):
                filler = proj_qkv(h + 1) if h + 1 < H else None
                ret_loop(h, filler)
                if filler is not None:
                    for _ in filler:
                        pass
                    proj_g_kz(h + 1)
            ps_n[0] = 7
            for mp in range(4):
                Wg, bWg = wload("w_in", 36 + mp)
                for q in range(2):
                    m = mp * 2 + q
                    Wr, bWr = wload("w_ret_o", m)
                    pg, bpg = next_ps()
                    for kt in range(8):
                        mm(pg[:], Wg[:, kt, q * 128:(q + 1) * 128], HB[:, kt, :], kt == 0, kt == 7, [bWg, bHB[kt]], [bpg])
                    act(FC[:, q, :], pg[:], AF.Sigmoid, [bpg, bPAR], [bFC[q]], bias=pc("bg1", m))
                    pb, bpb = next_ps()
                    for kt in range(16):
                        mm(pb[:], Wr[:, kt, :], RTT[:, kt, :], kt == 0, kt == 15, [bWr, bRTT[kt]], [bpb])
                    tt(FC[:, 2 + q, :], pb[:], FC[:, q, :], ALU.mult, [bpb, bFC[q]], [bFC[2 + q]])
                    tt(BA[:, m, :], FC[:, 2 + q, :], FA[:, m, :], ALU.add, [bFC[2 + q], bFA[m]], [bBA[m]])

            if stop == 'C':
                break
            for mp in range(4):
                W, bW = wload("w_out", mp)
                for q in range(2):
                    m = mp * 2 + q
                    pb, bpb = next_ps()
                    for kt in range(8):
                        mm(pb[:], W[:, kt, q * 128:(q + 1) * 128], BA[:, kt, :], kt == 0, kt == 7, [bW, bBA[kt]], [bpb])
                    stt(HT32[:, m, :], HT32[:, m, :], float(ALPHA), pb[:], ALU.mult, ALU.add, [bHT[m], bpb], [bHT[m]])
                    act(BB[:, m, :], HT32[:, m, :], AF.Square, [bHT[m]], [bBB[m]])
                    act(RTT[:, m, :], HT32[:, m, :], AF.Copy, [bHT[m]], [bRTT[m]])
            ln_stats(RTT, bRTT, BB, bBB)
            for f in range(8):
                ln_norm(HT32, bHT, f)
                act(BA[:, f, :], HT32[:, f, :], AF.Identity, [bHT[f], bPAR], [bBA[f]], bias=pc("ln1_b", f), scale=pc("ln1_g", f))
                act(HT32[:, f, :], HT32[:, f, :], AF.Identity, [bHT[f], bPAR], [bHT[f]], bias=pc("ln1_b", f), scale=pc("ln1_g", f))
            H1 = BA
            bH1 = bBA

            if stop == 'D':
                break
            alias_barrier(hid_groups["head"], bHID)
            o_fdw, _ = PCOLS["fdw"]
            o_fdb, _ = PCOLS["fdb"]
            if first:
                S.op("dve", lambda e: e.memset(ZHT[:], 0.0), [], bZH)
            ZH = ZHT[:, :].rearrange("p (c t) -> p c t", c=44)
            deferred_halo = []
            deferred_mult = []
            for jp in range(11):
                Wg, bWg = wload("w_ffn_up", jp)
                Wv, bWv = wload("w_ffn_up", 11 + jp)
                for q in range(2):
                    j = jp * 2 + q
                    accs = []
                    for part, (W, bW) in enumerate(((Wg, bWg), (Wv, bWv))):
                        c = j if part == 0 else 22 + j
                        pb, bpb = next_ps()
                        for kt in range(8):
                            mm(pb[:], W[:, kt, q * 128:(q + 1) * 128], H1[:, kt, :], kt == 0, kt == 7, [bW, bH1[kt]], [bpb])
                        aslot = (j % 2) * 3 + part
                        A_ = FA[:, aslot, :]
                        w0 = PAR[:, o_fdw + c * 3 + 0:o_fdw + c * 3 + 1]
                        w1 = PAR[:, o_fdw + c * 3 + 1:o_fdw + c * 3 + 2]
                        w2 = PAR[:, o_fdw + c * 3 + 2:o_fdw + c * 3 + 3]
                        bb_ = PAR[:, o_fdb + c:o_fdb + c + 1]
                        act(A_, pb[:], AF.Identity, [bpb, bPAR], [bFA[aslot]], bias=bb_, scale=w2)
                        act(A_[:, 0:1], ZH[:, c, 1:2], AF.Identity, [bZH[c], bPAR, bFA[aslot]], [bFA[aslot]], bias=A_[:, 0:1], scale=w1)
                        for fn_ in deferred_halo:
                            fn_()
                        deferred_halo = []
                        stt(A_[:, 1:512], pb[:, 0:511], w1, A_[:, 1:512], ALU.mult, ALU.add, [bpb, bPAR, bFA[aslot]], [bFA[aslot]])
                        stt(A_[:, 2:512], pb[:, 0:510], w0, A_[:, 2:512], ALU.mult, ALU.add, [bpb, bPAR, bFA[aslot]], [bFA[aslot]])
                        stt(A_[:, 0:2], ZH[:, c, 0:2], w0, A_[:, 0:2], ALU.mult, ALU.add, [bZH[c], bPAR, bFA[aslot]], [bFA[aslot]])
                        deferred_halo.append((lambda c_, pb_, bpb_: lambda: act(ZH[:, c_, 0:2], pb_[:, 510:512], AF.Copy, [bpb_], [bZH[c_]]))(c, pb, bpb))
                        accs.append((A_, bFA[aslot]))
                    (Ag, bAg), (Av, bAv) = accs
                    sslot = (j % 2) * 3 + 2
                    act(FA[:, sslot, :], Ag, AF.Silu, [bAg], [bFA[sslot]])
                    for fn_ in deferred_mult:
                        fn_()
                    deferred_mult = [(lambda j_, ss_, Av_, bAv_: lambda: tt(HIDV[:, j_, :], FA[:, ss_, :], Av_, ALU.mult, [bFA[ss_], bAv_], [bHID[j_]]))(j, sslot, Av, bAv)]
            for fn_ in deferred_halo + deferred_mult:
                fn_()

            dma("pld", PL[:], p_d[t0:t0 + TB, :].rearrange("(t p) c -> p t c", p=128), [], [bPL])
            for kt in range(2):
                pt_, bpt = next_ps()
                for tti in range(4):
                    tr(pt_[:, tti * 128:(tti + 1) * 128], PL[:, tti, kt * 128:(kt + 1) * 128], IDF[:], [bPL, bID], [bpt])
                act(PT[:, kt, :], pt_[:], AF.Copy, [bpt], [bPT])
            for mp in range(4):
                Wg, bWg = wload("w_ple_gate", mp)
                Wp, bWp = wload("w_ple", mp)
                for q in range(2):
                    m = mp * 2 + q
                    pg, bpg = next_ps()
                    for kt in range(8):
                        mm(pg[:], Wg[:, kt, q * 128:(q + 1) * 128], H1[:, kt, :], kt == 0, kt == 7, [bWg, bH1[kt]], [bpg])
                    act(FC[:, q, :], pg[:], AF.Sigmoid, [bpg, bPAR], [bFC[q]], bias=pc("bpg", m))
                    pp, bpp = next_ps()
                    for kt in range(2):
                        mm(pp[:], Wp[:, kt, q * 128:(q + 1) * 128], PT[:, kt, :], kt == 0, kt == 1, [bWp, bPT], [bpp])
                    tt(FC[:, 2 + q, :], pp[:], FC[:, q, :], ALU.mult, [bpp, bFC[q]], [bFC[2 + q]])
                    pf, bpf = next_ps()
                    for half in range(2):
                        Wd, bWd = wload("w_ffn_down", m * 2 + half)
                        for kt in range(11):
                            j = half * 11 + kt
                            mm(pf[:], Wd[:, kt, :], HIDV[:, j, :], j == 0, j == 21, [bWd, bHID[j]], [bpf])
                    stt(HT32[:, m, :], HT32[:, m, :], float(ALPHA), pf[:], ALU.mult, ALU.add, [bHT[m], bpf], [bHT[m]])
                    tt(HT32[:, m, :], HT32[:, m, :], FC[:, 2 + q, :], ALU.add, [bHT[m], bFC[2 + q]], [bHT[m]])
                    act(BB[:, m, :], HT32[:, m, :], AF.Square, [bHT[m]], [bBB[m]])
                    act(RTT[:, m, :], HT32[:, m, :], AF.Copy, [bHT[m]], [bRTT[m]])
            ln_stats(RTT, bRTT, BB, bBB)
            for f in range(8):
                ln_norm(HT32, bHT, f)
                act(HT32[:, f, :], HT32[:, f, :], AF.Identity, [bHT[f], bPAR], [bHT[f]], bias=pc("ln2_b", f), scale=pc("ln2_g", f))
            for tti in range(4):
                s = tti % 2
                for half in range(2):
                    pt_, bpt = next_ps()
                    for q in range(4):
                        f = half * 4 + q
                        tr(pt_[:, q * 128:(q + 1) * 128], HT32[:, f, tti * 128:(tti + 1) * 128], IDF[:], [bHT[f], bID], [bpt])
                    if half == 0:
                        act(OS[:, s, 0:512], pt_[:], AF.Copy, [bpt], [bOS[s]])
                    else:
                        cp(OS[:, s, 512:1024], pt_[:], [bpt], [bOS[s]])
                r0 = t0 + tti * 128
                out_toks[s] = dma("o%d" % s, out_d[r0:r0 + 128, :], OS[:, s, :], [bOS[s]], [bOUT])
            alias_barrier(bHID, bU)
            alias_barrier(bHID, hid_groups["head"])

        S.wait_all("sp", list(out_toks.values()))
        S.emit()
    return nc


_CACHE = {}


def kernel(**inp):
    inp = {k: np.asarray(v) for k, v in inp.items()}
    P, mk = _pack_params(inp)
    packs = {
        "w_in": _pack_w(np.asarray(inp["w_in"][0], np.float32), 8, 256),
        "w_conv_o": _pack_w(np.asarray(inp["w_conv_o"][0], np.float32), 8, 256),
        "w_out": _pack_w(np.asarray(inp["w_out"][0], np.float32), 8, 256),
        "w_ple_gate": _pack_w(np.asarray(inp["w_ple_gate"][0], np.float32), 8, 256),
        "w_ffn_up": _pack_w(np.asarray(inp["w_ffn_up"][0], np.float32), 8, 256),
        "w_ret_o": _pack_w(np.asarray(inp["w_ret_o"][0], np.float32), 16, 128),
        "w_ffn_down": _pack_wdown(np.asarray(inp["w_ffn_down"][0], np.float32)),
        "w_ple": _pack_w(np.asarray(inp["w_ple"][0], np.float32), 2, 256),
    }
    ident = np.eye(128, dtype=np.float32)
    x = np.asarray(inp["x"], np.float32)
    pos = np.asarray(inp["positions"], np.int32)
    pp = np.asarray(inp["p"][0], np.float32)
    if "nc" not in _CACHE:
        _CACHE["nc"] = build_nc()
    nc = _CACHE["nc"]
    in_maps = []
    for c in range(NCORES):
        m = {
            "x": np.ascontiguousarray(x[2 * c:2 * c + 2].reshape(TOK, D)),
            "pos": np.ascontiguousarray(pos[2 * c:2 * c + 2].reshape(1, TOK)),
            "p": np.ascontiguousarray(pp[2 * c:2 * c + 2].reshape(TOK, 256)),
            "params": P,
            "mask": np.ascontiguousarray(mk.reshape(128, H * 128)),
            "ident": ident,
        }
        m.update(packs)
        in_maps.append(m)
    res = run_bass_kernel_spmd(nc, in_maps, core_ids=list(range(NCORES)))
    out = np.stack([np.asarray(r["out"], np.float32).reshape(2, SEQ, D) for r in res.results], axis=0)
    return out.reshape(16, SEQ, D)
```
